# Optimizing a Trainium2 kernel written in Bass

```python
import jax, jax.numpy as jnp
from jax import lax
import numpy as np

D_MODEL = 1024
BATCH = 8
SEQ = 4096
DEPTH = 1

PLE_DIM = 256
D_MIX = D_MODEL
D_CONV = D_MIX // 2
D_REC = D_MIX - D_CONV
CONV_GROUPS = 8
CONV_GROUP_DIM = D_CONV // CONV_GROUPS
CONV_WIDTH = 31
REC_HEAD_DIM = 128
N_REC_HEADS = D_REC // REC_HEAD_DIM
CHUNK = 64
D_FF = 4 * D_MODEL
D_IN_PROJ = 2 * D_CONV + 4 * D_REC
EPS = 1e-6

kernel_name = "hymba_conformer_hgrn2_hybrid"


def rmsnorm(x, g):
    xf = x.astype(jnp.float32)
    y = xf * lax.rsqrt(jnp.mean(xf * xf, axis=-1, keepdims=True) + EPS)
    return (y * g.astype(jnp.float32)).astype(x.dtype)


def group_layernorm(x, g, b):
    Bs, S, C = x.shape
    xf = x.astype(jnp.float32).reshape(Bs, S, CONV_GROUPS, CONV_GROUP_DIM)
    mu = jnp.mean(xf, axis=-1, keepdims=True)
    var = jnp.mean(jnp.square(xf - mu), axis=-1, keepdims=True)
    y = ((xf - mu) * lax.rsqrt(var + EPS)).reshape(Bs, S, C)
    return (y * g.astype(jnp.float32) + b.astype(jnp.float32)).astype(x.dtype)


def causal_depthwise_conv(x, w, b):
    C = x.shape[-1]
    y = lax.conv_general_dilated(
        x, w[:, None, :].astype(x.dtype), window_strides=(1,),
        padding=[(CONV_WIDTH - 1, 0)], dimension_numbers=("NWC", "WIO", "NWC"),
        feature_group_count=C)
    return y + b.astype(x.dtype)


def hgrn2_chunkwise(q, k, v, logf):
    Bs, S, H, dk = q.shape
    dv = v.shape[-1]
    n = S // CHUNK

    def to_chunks(t):
        return t.reshape(Bs, n, CHUNK, H, t.shape[-1]).transpose(1, 0, 3, 2, 4)

    qc, kc, vc, gc = map(to_chunks, (q, k, v, logf))
    bc = jnp.cumsum(gc, axis=3)
    causal = jnp.tril(jnp.ones((CHUNK, CHUNK), dtype=bool))[:, :, None]

    def step(state, inp):
        q_, k_, v_, b_ = inp
        diff = b_[:, :, :, None, :] - b_[:, :, None, :, :]
        decay = jnp.exp(jnp.where(causal, diff, -jnp.inf))
        attn = jnp.einsum('bhtd,bhsd,bhtsd->bhts', q_, k_, decay)
        o = (jnp.einsum('bhts,bhsv->bhtv', attn, v_)
             + jnp.einsum('bhtd,bhdv->bhtv', q_ * jnp.exp(b_), state))
        b_last = b_[:, :, -1:, :]
        new_state = (jnp.exp(b_last[:, :, 0, :])[..., None] * state
                     + jnp.einsum('bhsd,bhsv->bhdv', k_ * jnp.exp(b_last - b_), v_))
        return new_state, o

    s0 = jnp.zeros((Bs, H, dk, dv), jnp.float32)
    _, o = lax.scan(step, s0, (qc, kc, vc, bc))
    return o.transpose(1, 0, 3, 2, 4).reshape(Bs, S, H, dv)


def setup_inputs(seed: int = 0) -> dict:
    key = jax.random.key(seed)
    ks = jax.random.split(key, 20)
    f32 = jnp.float32

    def nrm(k, shape, scale):
        return jax.random.normal(k, shape, f32) * scale

    def gain(k, shape):
        return 1.0 + 0.05 * jax.random.normal(k, shape, f32)

    return {
        "x": jax.random.normal(ks[0], (BATCH, SEQ, D_MODEL), f32),
        "p": jax.random.normal(ks[1], (DEPTH, BATCH, SEQ, PLE_DIM), f32),
        "norm_mix_g": gain(ks[2], (DEPTH, D_MODEL)),
        "w_in": nrm(ks[3], (DEPTH, D_MODEL, D_IN_PROJ), D_MODEL ** -0.5),
        "conv_w": nrm(ks[4], (DEPTH, CONV_WIDTH, D_CONV), CONV_WIDTH ** -0.5),
        "conv_b": nrm(ks[5], (DEPTH, D_CONV), 0.02),
        "conv_ln_g": gain(ks[6], (DEPTH, D_CONV)),
        "conv_ln_b": nrm(ks[7], (DEPTH, D_CONV), 0.02),
        "lb_logits": gain(ks[8], (DEPTH + 1, D_REC)),
        "rec_norm_g": gain(ks[9], (DEPTH, D_REC)),
        "w_out": nrm(ks[10], (DEPTH, D_MIX, D_MODEL), D_MIX ** -0.5),
        "norm_ffn_g": gain(ks[11], (DEPTH, D_MODEL)),
        "w_up": nrm(ks[12], (DEPTH, D_MODEL, D_FF), D_MODEL ** -0.5),
        "w_down": nrm(ks[13], (DEPTH, D_FF, D_MODEL), D_FF ** -0.5),
        "w_ple": nrm(ks[14], (DEPTH, PLE_DIM, D_MODEL), PLE_DIM ** -0.5),
        "ple_norm_g": gain(ks[15], (DEPTH, D_MODEL)),
        "w_ple_gate": nrm(ks[16], (DEPTH, D_MODEL, D_MODEL), D_MODEL ** -0.5),
        "final_norm_g": gain(ks[17], (D_MODEL,)),
    }


def reference(x, p, norm_mix_g, w_in, conv_w, conv_b, conv_ln_g, conv_ln_b, lb_logits,
              rec_norm_g, w_out, norm_ffn_g, w_up, w_down, w_ple, ple_norm_g, w_ple_gate,
              final_norm_g):
    Bs, S, _ = x.shape
    lower_bounds = jnp.cumsum(jax.nn.softmax(lb_logits.astype(jnp.float32), axis=0), axis=0)
    h = x
    for l in range(DEPTH):
        u = rmsnorm(h, norm_mix_g[l])
        z = u @ w_in[l]
        conv_a, conv_b_in, q, f_pre, i_in, g = jnp.split(
            z, np.cumsum([D_CONV, D_CONV, D_REC, D_REC, D_REC]), axis=-1)

        c = conv_a * jax.nn.sigmoid(conv_b_in)
        c = causal_depthwise_conv(c, conv_w[l], conv_b[l])
        y_conv = jax.nn.silu(group_layernorm(c, conv_ln_g[l], conv_ln_b[l]))

        lb = lower_bounds[l]
        f = lb + (1.0 - lb) * jax.nn.sigmoid(f_pre.astype(jnp.float32))
        logf = jnp.log(f)
        k = 1.0 - f
        shp = (Bs, S, N_REC_HEADS, REC_HEAD_DIM)
        o = hgrn2_chunkwise(q.astype(jnp.float32).reshape(shp), k.reshape(shp),
                            i_in.astype(jnp.float32).reshape(shp), logf.reshape(shp))
        o = o * lax.rsqrt(jnp.mean(o * o, axis=-1, keepdims=True) + EPS)
        o = o.reshape(Bs, S, D_REC) * rec_norm_g[l].astype(jnp.float32)
        y_rec = (o.astype(x.dtype) * jax.nn.silu(g))

        h = h + jnp.concatenate([y_conv, y_rec], axis=-1) @ w_out[l]

        v = rmsnorm(h, norm_ffn_g[l])
        h = h + jnp.square(jax.nn.relu(v @ w_up[l])) @ w_down[l]

        h = h + rmsnorm(p[l] @ w_ple[l], ple_norm_g[l]) * jax.nn.sigmoid(h @ w_ple_gate[l])
    return rmsnorm(h, final_norm_g)
```

```python
import math
from contextlib import ExitStack

import numpy as np
import concourse.bass as bass
import concourse.mybir as mybir
from concourse.bass_utils import run_bass_kernel_spmd

dt = mybir.dt
F32 = dt.float32
BF16 = dt.bfloat16
I32 = dt.int32
ALU = mybir.AluOpType
AF = mybir.ActivationFunctionType

D = 1024
DIN = 3072
DC = 512
DR = 512
NH = 4
DFF = 4096
PLE = 256
CW = 31
T = 512
EPS = 1e-6
MOFF = 44.0
NB = 3
SEQ = 4096
NCORES = 8


class _Op:
    __slots__ = ("eng", "fn", "deps", "signal", "idx", "chan", "chan_val", "wait_total")


class Prog:
    ENG = ["pe", "act", "dve", "pool", "sp"]

    def __init__(self, nc, es):
        self.nc = nc
        self.es = es
        self.ops = {e: [] for e in self.ENG}
        self.sems = {e: es.enter_context(nc.semaphore("sem_" + e)) for e in self.ENG}
        self.chans = {}
        self.last_w = {}
        self.readers = {}
        self.touch = {}
        self.nops = 0

    def op(self, eng, fn, reads=(), writes=(), chan=None, wait_total=False):
        o = _Op()
        o.eng = eng
        o.fn = fn
        o.deps = []
        o.signal = False
        o.idx = 0
        o.chan = chan
        o.chan_val = 0
        o.wait_total = wait_total
        if chan is not None:
            c = self.chans.get(chan)
            if c is None:
                c = self.chans[chan] = [
                    self.es.enter_context(self.nc.semaphore("ch%d" % len(self.chans))), 0]
            c[1] += 16
            o.chan_val = c[1]
        writes = list(writes) + [k for k in reads if k[0] == "ps"]
        reads = [k for k in reads if k[0] != "ps"]
        self.nops += 1
        for k in writes:
            if k[0] == "ps":
                self.touch[k[1]] = self.nops
        deps = {}
        for k in reads:
            w = self.last_w.get(k)
            if w is not None:
                deps.setdefault(id(w), [w, set()])[1].add("raw")
        for k in writes:
            w = self.last_w.get(k)
            if w is not None:
                deps.setdefault(id(w), [w, set()])[1].add("waw")
            for r in self.readers.get(k, ()):
                deps.setdefault(id(r), [r, set()])[1].add("war")
        for d, kinds in deps.values():
            if d is o:
                continue
            if d.chan is None and d.eng == eng:
                if eng == "pe":
                    continue
            o.deps.append(d)
            if d.chan is None:
                d.signal = True
        for k in reads:
            self.readers.setdefault(k, []).append(o)
        for k in writes:
            self.last_w[k] = o
            self.readers[k] = []
        self.ops[eng].append(o)
        return o

    def emit(self):
        for e in self.ENG:
            n = 0
            for o in self.ops[e]:
                if o.chan is None and o.signal:
                    n += 1
                    o.idx = n
        handles = None
        with self.nc.Block() as blk:
            starters = {"pe": blk.tensor, "act": blk.scalar, "dve": blk.vector,
                        "pool": blk.gpsimd, "sp": blk.sync}
            for e in self.ENG:
                def body(h, e=e):
                    known = {}
                    for o in self.ops[e]:
                        for d in o.deps:
                            if d.chan is not None:
                                c = self.chans[d.chan]
                                sem, val = c[0], (c[1] if d.wait_total else d.chan_val)
                            else:
                                sem, val = self.sems[d.eng], d.idx
                            key = sem.num
                            if known.get(key, 0) < val:
                                h.wait_ge(sem, val)
                                known[key] = val
                        ins = o.fn(h)
                        if o.chan is not None:
                            ins.then_inc(self.chans[o.chan][0], 16)
                        elif o.signal:
                            ins.then_inc(self.sems[e], 1)
                    if e == "sp":
                        for name, c in self.chans.items():
                            if known.get(c[0].num, 0) < c[1]:
                                h.wait_ge(c[0], c[1])
                starters[e](body)


class _Stop(Exception):
    pass


STOP = None


def _stop(name):
    if STOP == name:
        raise _Stop()


def _ap(t, off, dims):
    pstep = 1
    for s in list(t.shape)[1:]:
        pstep *= int(s)
    return bass.AP(t, off, [[pstep, 128]] + [list(d) for d in dims])


def build(S):
    NST = S // T
    NT = S // 128
    nc = bass.Bass("TRN2", target_bir_lowering=False)

    def din(name, shape):
        return nc.dram_tensor(name, shape, F32, kind="ExternalInput").ap()

    x_d = din("x", [S, D])
    p_d = din("p", [S, PLE])
    gmix_d = din("norm_mix_g", [D])
    win_d = din("w_in", [D, DIN])
    convw_d = din("conv_w", [CW, DC])
    convb_d = din("conv_b", [DC])
    lng_d = din("conv_ln_g", [DC])
    lnb_d = din("conv_ln_b", [DC])
    lbl_d = din("lb_logits", [2, DR])
    grec_d = din("rec_norm_g", [DR])
    wout_d = din("w_out", [D, D])
    gffn_d = din("norm_ffn_g", [D])
    wup_d = din("w_up", [D, DFF])
    wdown_d = din("w_down", [DFF, D])
    wple_d = din("w_ple", [PLE, D])
    gple_d = din("ple_norm_g", [D])
    wgate_d = din("w_ple_gate", [D, D])
    gfin_d = din("final_norm_g", [D])
    out_d = nc.dram_tensor("out", [S, D], F32, kind="ExternalOutput").ap()

    def dscr(name, shape):
        return nc.dram_tensor(name, shape, BF16, kind="Internal").ap()

    win_b = dscr("w_in_b", [D, DIN])
    wout_b = dscr("w_out_b", [D, D])
    wup_b = dscr("w_up_b", [D, DFF])
    wdown_b = dscr("w_down_b", [DFF, D])
    wple_b = dscr("w_ple_b", [PLE, D])
    wgate_b = dscr("w_gate_b", [D, D])
    NBLK = 32
    wscr = dscr("wscr", [NBLK, 128, 4096])
    blk_idx = {}

    def blk(name, r0, c0):
        k = (name, r0, c0)
        if k not in blk_idx:
            blk_idx[k] = len(blk_idx)
            assert blk_idx[k] < NBLK
        return blk_idx[k]

    es = ExitStack()
    with es:
        def sb(name, shape, dtype):
            return es.enter_context(nc.sbuf_tensor(name, shape, dtype))

        ring = sb("ring", [128, NB, 8, 512], BF16)
        big = sb("big", [128, 4096], F32)
        hbuf = sb("hbuf", [128, 2, 4, 1024], F32)
        pin = sb("pin", [128, 4, 256], F32)
        xn = sb("xn", [128, 2, 1024], BF16)
        xT2 = sb("xT", [128, 16, 512], BF16)
        sqj = sb("sqj", [128, 1024], BF16)
        Fb = sb("Fb", [128, 10, 512], F32)
        gsilu = sb("gsilu", [128, 4, 512], F32)
        cv32 = sb("cv32", [128, 2, 512], F32)
        cvb = sb("cvb", [128, 2, 512], BF16)
        csq = sb("csq", [128, 2, 512], BF16)
        cT = sb("cT", [128, 4, 544], BF16)
        qT = sb("qT", [128, 4, 512], BF16)
        kT = sb("kT", [128, 4, 512], BF16)
        vtok = sb("vtok", [128, 4, 512], BF16)
        kp = sb("kp", [128, 4, 512], BF16)
        attnm = sb("attnm", [128, 1, 512], BF16)
        yrec = sb("yrec", [128, 1, 512], BF16)
        Sst = sb("Sst", [128, 4, 128], F32)
        Sbf = sb("Sbf", [128, 4, 512], BF16)
        ebl4 = sb("ebl4", [128, 4, 4], F32)
        pb = sb("pb", [128, 2, 256], BF16)
        pT = sb("pT", [128, 2, 512], BF16)
        diag = sb("diag", [128, CW * 4, 128], BF16)
        ident = sb("ident", [128, 128], BF16)
        ones32 = sb("ones32", [128, 128], F32)
        maskT = sb("maskT", [128, 128], F32)
        blkm = sb("blkm", [128, 128], BF16)
        gple_bc = sb("gple_bc", [128, 1024], F32)
        gfin_bc = sb("gfin_bc", [128, 1024], F32)
        gmix = sb("gmix", [128, 8], F32)
        gout = sb("gout", [128, 8], F32)
        lbv = sb("lbv", [128, 4], F32)
        omlb = sb("omlb", [128, 4], F32)
        vstA = gsilu[:, 0, 0:128]
        cwst = gsilu[:, 0, 128:256]
        vecs = sb("vecs", [128, 32], F32)
        cwt = sb("cwt", [128, CW * 4], F32)
        ident32 = gsilu[:, 0, 256:384]
        gffn = vecs[:, 0:8]
        convb = vecs[:, 12:16]
        lng = vecs[:, 16:20]
        lnb = vecs[:, 20:24]
        lbl = vecs[:, 24:32].rearrange("p (r c) -> p r c", c=4)
        cw = cwt[:].rearrange("p (k m) -> p k m", m=4)
        dec = sb("dec", [128, 4, 4], F32)
        stat = sb("stat", [128, 8, 16], F32)
        nt = sb("nt", [128, 4, 16], F32)

        ps = [es.enter_context(nc.psum_tensor("ps%d" % i, [128, 512], F32)) for i in range(8)]
        P = Prog(nc, es)

        aT = big[:].bitcast(BF16).rearrange("p (a b) -> p a b", b=512)
        stage0 = big[:].rearrange("p (a b) -> p a b", b=512)
        stage1 = Fb[:, 0:8, :]

        held = set()

        def bank(hold=False):
            cands = [b for b in range(8) if b not in held]
            b = min(cands, key=lambda b: P.touch.get(b, -b))
            P.nops += 1
            P.touch[b] = P.nops
            if hold:
                held.add(b)
            return b

        def unhold(b):
            held.discard(b)

        def dma(out, in_, reads, writes, chan, wait_total=False, slow=False):
            if slow:
                fn = lambda h: h.dma_start(out=out, in_=in_, allow_slow_non_contiguous=True)
            else:
                fn = lambda h: h.dma_start(out=out, in_=in_)
            return P.op("sp", fn, reads, writes, chan=chan, wait_total=wait_total)

        def vec_load(tile, src, ncol):
            for c in range(ncol):
                dma(tile[:, c:c + 1], src[c * 128:(c + 1) * 128].rearrange("(p o) -> p o", o=1),
                    [], [("c", tile.name, c)], chan="misc", wait_total=True)

        def ck(name, n):
            return [("c", name, c) for c in range(n)]

        def newton(src_ap, n, scale, dst_ap, rkeys, wkeys):
            tx = nt[:, 0, 0:n]
            P.op("act", lambda h: h.activation(out=tx, in_=src_ap, func=AF.Ln, scale=float(scale), bias=float(EPS)),
                 rkeys, [("nt", 0)])
            P.op("act", lambda h: h.activation(out=dst_ap, in_=tx, func=AF.Exp, scale=-0.5), [("nt", 0)], wkeys)

        try:
            P.op("pool", lambda h: h.memset(ones32[:], 1.0), [], [("c", "ones32")])
            P.op("pool", lambda h: h.affine_select(out=maskT[:], in_=ones32[:], pattern=[[1, 128]],
                                                   compare_op=ALU.is_ge, fill=0.0, base=0, channel_multiplier=-1),
                 [("c", "ones32")], [("c", "maskT")])
            P.op("pool", lambda h: h.affine_select(out=ident[:], in_=ones32[:], pattern=[[1, 128]],
                                                   compare_op=ALU.is_equal, fill=0.0, base=0, channel_multiplier=-1),
                 [("c", "ones32")], [("c", "ident")])
            P.op("pool", lambda h: h.affine_select(out=ident32[:], in_=ones32[:], pattern=[[1, 128]],
                                                   compare_op=ALU.is_equal, fill=0.0, base=0, channel_multiplier=-1),
                 [("c", "ones32")], [("c", "ident32")])
            P.op("pool", lambda h: h.memset(blkm[:], 0.0), [], [("c", "blkm")])
            P.op("pool", lambda h: h.memset(blkm[0:64, 0:64], 1.0 / 64), [("c", "blkm")], [("c", "blkm")])
            P.op("pool", lambda h: h.memset(blkm[64:128, 64:128], 1.0 / 64), [("c", "blkm")], [("c", "blkm")])
            P.op("pool", lambda h: h.memset(cT[:], 0.0), [], [("cT", m) for m in range(4)])
            P.op("pool", lambda h: h.memset(Sst[:], 0.0), [], [("S", hd) for hd in range(4)])
            P.op("pool", lambda h: h.memset(Sbf[:], 0.0), [], [("Sbf", t_) for t_ in range(4)])
            P.op("pool", lambda h: h.memset(gout[:], 1.0), [], ck("gout", 8))

            _stop("consts")
            for c in range(8):
                dma(gmix[:, c:c + 1], gmix_d[c * 128:(c + 1) * 128].rearrange("(p o) -> p o", o=1),
                    [], [("c", "gmix", c)], chan="misc0", wait_total=True)

            def late_consts():
                rows = [(gffn_d, 0, 8, None), (grec_d, 8, 4, None), (convb_d, 12, 4, None), (lng_d, 16, 4, None),
                        (lnb_d, 20, 4, None), (lbl_d, 24, 8, "2d")]
                for src, r0, n, kind in rows:
                    if kind == "2d":
                        src_ap = src.rearrange("r (c p) -> (r c) p", p=128)
                    else:
                        src_ap = src.rearrange("(c p) -> c p", p=128)
                    dma(vstA[r0:r0 + n, :], src_ap, [], [("gsilu", 0, "v", r0)], chan="misc", wait_total=True)
                dma(cwst[0:CW * 4, :], convw_d.rearrange("k (m p) -> (k m) p", p=128), [], [("gsilu", 0, "c")],
                    chan="misc", wait_total=True)
                dma(gple_bc[:], bass.AP(gple_d.tensor, 0, [[0, 128], [1, 1024]]), [], [("c", "gple")],
                    chan="misc", wait_total=True)
                dma(gfin_bc[:], bass.AP(gfin_d.tensor, 0, [[0, 128], [1, 1024]]), [], [("c", "gfin")],
                    chan="misc", wait_total=True)
                bv = bank()
                P.op("pe", lambda h: h.transpose(out=ps[bv][:, 0:32], in_=vstA[0:32, :], identity=ident32[0:32, 0:32]),
                     [("gsilu", 0, "v", r0) for _, r0, _, _ in rows] + [("c", "ident32")], [("ps", bv)])
                vkeys = ck("gffn", 8) + ck("convb", 4) + ck("lng", 4) + ck("lnb", 4) + ck("lbl", 8) + [("c", "grec")]
                P.op("dve", lambda h: h.tensor_copy(out=vecs[:], in_=ps[bv][:, 0:32]), [("ps", bv)], vkeys)
                P.op("dve", lambda h: h.tensor_copy(out=gout[:, 4:8], in_=vecs[:, 8:12]), [("c", "grec")],
                     [("c", "gout", 4 + c) for c in range(4)])
                bc = bank()
                P.op("pe", lambda h: h.transpose(out=ps[bc][:, 0:CW * 4], in_=cwst[0:CW * 4, :],
                                                 identity=ident32[0:CW * 4, 0:CW * 4]),
                     [("gsilu", 0, "c"), ("c", "ident32")], [("ps", bc)])
                P.op("dve", lambda h: h.tensor_copy(out=cwt[:], in_=ps[bc][:, 0:CW * 4]), [("ps", bc)],
                     [("c", "cw", k) for k in range(CW)])
                P.op("dve", lambda h: h.tensor_tensor(out=omlb[:], in0=lbl[:, 0, :], in1=lbl[:, 1, :],
                                                      op=ALU.subtract),
                     ck("lbl", 8), [("c", "omlb")])
                P.op("act", lambda h: h.activation(out=lbv[:], in_=omlb[:], func=AF.Sigmoid),
                     [("c", "omlb")], [("c", "lbv")])
                P.op("dve", lambda h: h.tensor_scalar(out=omlb[:], in0=lbv[:], scalar1=-1.0, scalar2=1.0,
                                                      op0=ALU.mult, op1=ALU.add),
                     [("c", "lbv")], [("c", "omlb")])
                for k in range(CW):
                    for m in range(4):
                        P.op("pool", lambda h, k=k, m=m: h.tensor_scalar(
                            out=diag[:, k * 4 + m, :], in0=ident[:], scalar1=cw[:, k, m:m + 1], scalar2=1.0,
                            op0=ALU.mult, op1=ALU.mult),
                            [("c", "ident"), ("c", "cw", k)], [("c", "diag")])

            fmap = {"w_in_b": (win_d, gmix, "gmix"), "w_out_b": (wout_d, gout, "gout"),
                    "w_up_b": (wup_d, gffn, "gffn"), "w_down_b": (wdown_d, None, None),
                    "w_ple_b": (wple_d, None, None), "w_gate_b": (wgate_d, None, None)}
            first_pass = [True]
            cast_i = [0]

            def cast_op(o_ap, i_ap, sc, rds, wrs):
                eng = ("act", "dve")[cast_i[0] % 2]
                cast_i[0] += 1
                if eng == "act":
                    if sc is not None:
                        fn = lambda h: h.activation(out=o_ap, in_=i_ap, func=AF.Copy, scale=sc)
                    else:
                        fn = lambda h: h.activation(out=o_ap, in_=i_ap, func=AF.Copy)
                else:
                    s1 = sc if sc is not None else 1.0
                    fn = lambda h: h.tensor_scalar(out=o_ap, in0=i_ap, scalar1=s1, scalar2=1.0,
                                                   op0=ALU.mult, op1=ALU.mult)
                P.op(eng, fn, rds, wrs)

            def wload_first(src, r0, nkc, c0, ncols, slot):
                fsrc, sct, sck = fmap[src.tensor.name]
                rkey = ("ring", slot)
                if ncols == 512:
                    for half in range(2):
                        stg = hbuf[:, 1, 2 * half:2 * half + 2, :].rearrange("p a (b n) -> p (a b) n", n=512)
                        hk = [("h", 1, 2 * half), ("h", 1, 2 * half + 1)]
                        rr = r0 + half * 512
                        dma(stg, fsrc[rr:rr + 512, c0:c0 + 512].rearrange("(k p) n -> p k n", p=128),
                            [], hk, chan=("stg", half))
                        for k4 in range(4):
                            kc = half * 4 + k4
                            gi = r0 // 128 + kc
                            sc = sct[:, gi:gi + 1] if sct is not None else None
                            rds = hk + ([("c", sck, gi)] if sct is not None else [])
                            cast_op(ring[:, slot, kc, :], stg[:, k4, :], sc, rds, [(rkey, kc)])
                    dma(wscr[blk(src.tensor.name, r0, c0)], ring[:, slot].rearrange("p k n -> p (k n)"),
                        [(rkey, kc) for kc in range(8)], [("scr", src.tensor.name, r0, c0)], chan=("ring", slot))
                else:
                    stg = hbuf[:, 1, 0:2, :]
                    hk = [("h", 1, 0), ("h", 1, 1)]
                    dma(stg, fsrc[0:256, 0:1024].rearrange("(k p) n -> p k n", p=128), [], hk, chan=("stg", 0))
                    pv = ring[:, slot].rearrange("p k n -> p (k n)")[:, 0:2048].rearrange("p (k n) -> p k n", n=1024)
                    for kc in range(2):
                        cast_op(pv[:, kc, :], stg[:, kc, :], None, hk, [(rkey, kc)])
                    dma(wscr[blk(src.tensor.name, 0, 0)][:, 0:2048],
                        ring[:, slot].rearrange("p k n -> p (k n)")[:, 0:2048],
                        [(rkey, kc) for kc in range(2)],
                        [("scr", src.tensor.name, 0, 0), ("scr", src.tensor.name, 0, 512)], chan=("ring", slot))

            ring_i = [0]
            prefetched = {}
            pending_stores = []

            def deferred_stores():
                while pending_stores:
                    r0, hb_, tt_ = pending_stores.pop(0)
                    dma(out_d[r0:r0 + 128, :], hbuf[:, hb_, tt_, :], [("h", hb_, tt_)], [], chan=("o", hb_, tt_))

            def wload(src, r0, nkc, c0, ncols=512):
                pk = (src.tensor.name, r0, c0)
                if pk in prefetched:
                    return prefetched.pop(pk)
                i = ring_i[0]
                ring_i[0] += 1
                slot = i % NB
                rkey = ("ring", slot)
                if first_pass[0]:
                    wload_first(src, r0, nkc, c0, ncols, slot)
                    return slot
                nel = nkc * ncols
                src_ap = wscr[blk(src.tensor.name, r0, c0)][:, 0:nel]
                dst_ap = ring[:, slot].rearrange("p k n -> p (k n)")[:, 0:nel]
                dma(dst_ap, src_ap, [("scr", src.tensor.name, r0, c0 + j * 512) for j in range(ncols // 512)],
                    [(rkey, kc) for kc in range(8)], chan=("ring", slot))
                return slot

            def rkeys(slot, nkc=8):
                return [(("ring", slot), kc) for kc in range(nkc)]

            def transposes(src_fn, nblk, dst_fn, src_keys, dst_keys, evac_eng):
                b = bank()
                pv = ps[b][:].bitcast(BF16)

                def fn(h):
                    r = None
                    for c in range(nblk):
                        r = h.transpose(out=pv[:, c * 128:(c + 1) * 128], in_=src_fn(c), identity=ident[:])
                    return r
                P.op("pe", fn, src_keys + [("c", "ident")], [("ps", b)])
                src3 = pv[:, 0:nblk * 128].rearrange("p (c t) -> p c t", t=128)
                if evac_eng == "act":
                    P.op("act", lambda h: h.activation(out=dst_fn(), in_=src3, func=AF.Copy), [("ps", b)], dst_keys)
                else:
                    P.op("dve", lambda h: h.tensor_copy(out=dst_fn(), in_=src3), [("ps", b)], dst_keys)

            def load_x(st):
                hb = st % 2
                for tt in range(4):
                    r0 = st * T + tt * 128
                    dma(hbuf[:, hb, tt, :], x_d[r0:r0 + 128, :], [], [("h", hb, tt)], chan=("x", hb, tt))

            load_x(0)

            def do_st(st):
                hb = st % 2
                first_pass[0] = (st == 0)
                for tt in range(4):
                    r0 = st * T + tt * 128
                    dma(pin[:, tt, :], p_d[r0:r0 + 128, :], [], [("pin", tt)], chan=("p", tt))

                def norm_stats(statrow, hbx):
                    for tt in range(4):
                        P.op("act", lambda h, tt=tt: h.activation(out=sqj[:], in_=hbuf[:, hbx, tt, :], func=AF.Square,
                                                                  accum_out=stat[:, statrow, tt:tt + 1]),
                             [("h", hbx, tt)], [("stat", statrow, tt), ("sqj",)])
                    newton(stat[:, statrow, 0:4], 4, 1.0 / D, stat[:, statrow, 4:8],
                           [("stat", statrow, tt) for tt in range(4)], [("stat", statrow, "r")])

                def norm_apply(statrow, evac_alt, hbx):
                    for tt in range(4):
                        xb = tt % 2
                        P.op("pool", lambda h, tt=tt, xb=xb: h.tensor_scalar(
                            out=xn[:, xb, :], in0=hbuf[:, hbx, tt, :], scalar1=stat[:, statrow, 4 + tt:5 + tt],
                            scalar2=1.0, op0=ALU.mult, op1=ALU.mult),
                            [("h", hbx, tt), ("stat", statrow, "r")], [("xn", xb)])
                        transposes(lambda c, xb=xb: xn[:, xb, c * 128:(c + 1) * 128], 8,
                                   lambda tt=tt: xT2[:, hbx * 8:(hbx + 1) * 8, tt * 128:(tt + 1) * 128],
                                   [("xn", xb)], [("xT", hbx, kc, tt) for kc in range(8)],
                                   "act" if (tt + evac_alt) % 2 == 0 else "dve")

                def norm_to_xT(statrow, evac_alt):
                    norm_stats(statrow, hb)
                    norm_apply(statrow, evac_alt, hb)

                if st == 0:
                    norm_to_xT(0, 0)
                _stop("A")
                xT = xT2[:, hb * 8:(hb + 1) * 8, :]
                xT_all = [("xT", hb, kc, tt) for kc in range(8) for tt in range(4)]

                def mm_fm(slot, m, b, nkc=8):
                    def fn(h):
                        r = None
                        for kc in range(nkc):
                            r = h.matmul(ps[b][:], lhsT=ring[:, slot, kc, m * 128:(m + 1) * 128], rhs=xT[:, kc, :],
                                         start=(kc == 0), stop=(kc == nkc - 1))
                        return r
                    P.op("pe", fn, rkeys(slot) + xT_all, [("ps", b)])

                def mm_tm(slot, tt, b, nkc=8, src=None, ncols=512, c0=0, slotview=None):
                    srcT = xT if src is None else src

                    def fn(h):
                        r = None
                        for kc in range(nkc):
                            rhs = ring[:, slot, kc, :] if slotview is None else slotview[:, kc, c0:c0 + ncols]
                            r = h.matmul(ps[b][:], lhsT=srcT[:, kc, tt * 128:(tt + 1) * 128], rhs=rhs,
                                         start=(kc == 0), stop=(kc == nkc - 1))
                        return r
                    keys = [("xT", hb, kc, tt) for kc in range(nkc)] if src is None else [("pT", tt)]
                    P.op("pe", fn, rkeys(slot) + keys, [("ps", b)])

                s_f = wload(win_b, 0, 8, 1536)
                s_q = wload(win_b, 0, 8, 1024)
                if st == 0:
                    late_consts()
                for hd in range(4):
                    b = bank()
                    mm_fm(s_f, hd, b)
                    P.op("act", lambda h, hd=hd, b=b: h.activation(out=Fb[:, hd, :], in_=ps[b][:], func=AF.Sigmoid),
                         [("ps", b)], [("F", hd)])
                    P.op("pool", lambda h, hd=hd: h.tensor_scalar(out=Fb[:, hd, :], in0=Fb[:, hd, :],
                                                                 scalar1=omlb[:, hd:hd + 1], scalar2=lbv[:, hd:hd + 1],
                                                                 op0=ALU.mult, op1=ALU.add),
                         [("F", hd), ("c", "omlb"), ("c", "lbv")], [("F", hd)])
                qbanks = []
                for hd in range(4):
                    b = bank(hold=True)
                    mm_fm(s_q, hd, b)
                    qbanks.append(b)
                s_cb = wload(win_b, 0, 8, 512)
                s_ca = wload(win_b, 0, 8, 0)

                def conv_ba(m):
                    sg = 5 + 3 * (m % 2)
                    b = bank()
                    mm_fm(s_cb, m, b)
                    P.op("act", lambda h, b=b, sg=sg: h.activation(out=Fb[:, sg, :], in_=ps[b][:], func=AF.Sigmoid),
                         [("ps", b)], [("F", sg)])
                    b2 = bank()
                    mm_fm(s_ca, m, b2)
                    P.op("dve", lambda h, m=m, b2=b2, sg=sg: h.tensor_tensor(out=cT[:, m, 30:542], in0=ps[b2][:],
                                                                             in1=Fb[:, sg, :], op=ALU.mult),
                         [("ps", b2), ("F", sg)], [("cT", m)])

                def g_tile(tt, s_g):
                    b = bank()
                    mm_tm(s_g, tt, b)
                    xk = []
                    if st == 0 and tt == 0:
                        xk = [("gsilu", 0, "v", r0_) for r0_ in (0, 8, 12, 16, 20, 24)] + [("gsilu", 0, "c"),
                                                                                              ("c", "ident32")]
                    P.op("act", lambda h, tt=tt, b=b: h.activation(out=gsilu[:, tt, :], in_=ps[b][:], func=AF.Sigmoid),
                         [("ps", b)], [("gsilu", tt)] + xk)
                    P.op("dve", lambda h, tt=tt, b=b: h.tensor_tensor(out=gsilu[:, tt, :], in0=ps[b][:],
                                                                      in1=gsilu[:, tt, :], op=ALU.mult),
                         [("ps", b), ("gsilu", tt)], [("gsilu", tt)])

                def i_tile(tt, s_i):
                    b = bank()
                    mm_tm(s_i, tt, b)
                    P.op("act", lambda h, tt=tt, b=b: h.activation(out=vtok[:, tt, :], in_=ps[b][:], func=AF.Copy),
                         [("ps", b)], [("v", tt)])

                def chain(hd):
                        base = 4 + (hd % 2) * 3
                        t1, t2, t3 = base, base + 1, base + 2
                        for tt in range(4):
                            P.op("dve", lambda h, hd=hd, tt=tt, t1=t1: h.tensor_tensor_scan(
                                out=Fb[:, t1, tt * 128:(tt + 1) * 128], data0=Fb[:, hd, tt * 128:(tt + 1) * 128],
                                data1=ones32[:], initial=float(math.exp(MOFF)), op0=ALU.mult, op1=ALU.mult),
                                [("F", hd), ("c", "ones32")], [("F", t1)])
                        P.op("dve", lambda h, t1=t1, t3=t3: h.reciprocal(out=Fb[:, t3, :], in_=Fb[:, t1, :]),
                             [("F", t1)], [("F", t3)])
                        P.op("act", lambda h, hd=hd, t1=t1: h.activation(
                            out=dec[:, hd, :], in_=_ap(Fb, t1 * 512 + 127, [[128, 4]]), func=AF.Copy,
                            scale=float(math.exp(-MOFF))),
                            [("F", t1)], [("dec", hd)])
                        P.op("act", lambda h, hd=hd, t1=t1: h.activation(
                            out=ebl4[:, hd, :], in_=_ap(Fb, t1 * 512 + 127, [[128, 4]]), func=AF.Copy),
                            [("F", t1)], [("ebl", hd)])
                        P.op("dve", lambda h, hd=hd, t3=t3: h.scalar_tensor_tensor(
                            out=Fb[:, t3, :], in0=Fb[:, hd, :], scalar=1.0, in1=Fb[:, t3, :], op0=ALU.subtract,
                            op1=ALU.mult),
                            [("F", hd), ("F", t3)], [("F", t3)])
                        P.op("act", lambda h, hd=hd, t3=t3: h.activation(out=kT[:, hd, :], in_=Fb[:, t3, :], func=AF.Copy,
                                                                         scale=-1.0),
                             [("F", t3)], [("kT", hd)])
                        qb = qbanks[hd]
                        P.op("dve", lambda h, hd=hd, qb=qb, t1=t1: h.tensor_tensor(out=qT[:, hd, :], in0=ps[qb][:],
                                                                                  in1=Fb[:, t1, :], op=ALU.mult),
                             [("ps", qb), ("F", t1)], [("qT", hd)])
                        unhold(qb)


                chain(0)
                conv_ba(0)
                conv_ba(1)
                chain(1)
                conv_ba(2)
                conv_ba(3)
                s_g = wload(win_b, 0, 8, 2560)
                chain(2)
                for tt in range(4):
                    g_tile(tt, s_g)
                s_i = wload(win_b, 0, 8, 2048)
                deferred_stores()
                if st + 1 < NST and st != 0:
                    load_x(st + 1)
                chain(3)
                for tt in range(4):
                    i_tile(tt, s_i)

                _stop("B4")
                def conv_chunk(m):
                    b = bank()

                    def fnc(h, m=m, b=b):
                        r = None
                        for k in range(CW):
                            r = h.matmul(ps[b][:], lhsT=diag[:, k * 4 + m, :], rhs=cT[:, m, k:k + 512],
                                         start=(k == 0), stop=(k == CW - 1))
                        return r
                    P.op("pe", fnc, [("cT", m), ("c", "diag")], [("ps", b)])
                    P.op("pool", lambda h, m=m: h.tensor_copy(out=cT[:, m, 0:30], in_=cT[:, m, 512:542]),
                         [("cT", m)], [("cT", m)])
                    _stop("C1")
                    if m == 1:
                        _stop("C1b")
                    cb2 = m % 2
                    P.op("act", lambda h, m=m, b=b, cb2=cb2: h.activation(out=cvb[:, cb2, :], in_=ps[b][:],
                                                                          func=AF.Identity, bias=convb[:, m:m + 1]),
                         [("ps", b), *ck("convb", 4)], [("cvb", cb2)])
                    P.op("act", lambda h, m=m, b=b, cb2=cb2: h.activation(out=csq[:, cb2, :], in_=ps[b][:],
                                                                          func=AF.Square, bias=convb[:, m:m + 1]),
                         [("ps", b), *ck("convb", 4)], [("csq", cb2)])
                    P.op("act", lambda h, m=m, b=b, cb2=cb2: h.activation(out=cv32[:, cb2, :], in_=ps[b][:],
                                                                          func=AF.Identity, bias=convb[:, m:m + 1]),
                         [("ps", b), *ck("convb", 4)], [("cv32", cb2)])
                    _stop("C2")
                    bm = bank()
                    be = bank()
                    P.op("pe", lambda h, bm=bm, cb2=cb2: h.matmul(ps[bm][:], lhsT=blkm[:], rhs=cvb[:, cb2, :],
                                                                  start=True, stop=True),
                         [("cvb", cb2), ("c", "blkm")], [("ps", bm)])
                    P.op("pe", lambda h, be=be, cb2=cb2: h.matmul(ps[be][:], lhsT=blkm[:], rhs=csq[:, cb2, :],
                                                                  start=True, stop=True),
                         [("csq", cb2), ("c", "blkm")], [("ps", be)])
                    _stop("C3")
                    zs = m
                    t2 = 4 + (m % 2) * 3
                    P.op("dve", lambda h, zs=zs, bm=bm, cb2=cb2: h.tensor_tensor(out=Fb[:, zs, :], in0=cv32[:, cb2, :],
                                                                                 in1=ps[bm][:], op=ALU.subtract),
                         [("cv32", cb2), ("ps", bm)], [("F", zs)])
                    _stop("C3_1")
                    P.op("act", lambda h, t2=t2, bm=bm: h.activation(out=Fb[:, t2, :], in_=ps[bm][:], func=AF.Square),
                         [("ps", bm)], [("F", t2)])
                    _stop("C3_2")
                    P.op("dve", lambda h, t2=t2, be=be: h.tensor_tensor(out=Fb[:, t2, :], in0=ps[be][:],
                                                                        in1=Fb[:, t2, :], op=ALU.subtract),
                         [("ps", be), ("F", t2)], [("F", t2)])
                    _stop("C3_3")
                    P.op("act", lambda h, t2=t2: h.activation(out=Fb[:, t2, :], in_=Fb[:, t2, :], func=AF.Ln,
                                                              bias=float(EPS)),
                         [("F", t2)], [("F", t2)])
                    _stop("C3_4")
                    P.op("act", lambda h, t2=t2: h.activation(out=Fb[:, t2, :], in_=Fb[:, t2, :], func=AF.Exp,
                                                              scale=-0.5),
                         [("F", t2)], [("F", t2)])
                    _stop("C3_5")
                    P.op("pool", lambda h, zs=zs, t2=t2: h.tensor_tensor(out=Fb[:, zs, :], in0=Fb[:, zs, :],
                                                                         in1=Fb[:, t2, :], op=ALU.mult),
                         [("F", zs), ("F", t2)], [("F", zs)])
                    _stop("C3_6")
                    P.op("pool", lambda h, zs=zs, m=m: h.tensor_scalar(out=Fb[:, zs, :], in0=Fb[:, zs, :],
                                                                      scalar1=lng[:, m:m + 1], scalar2=lnb[:, m:m + 1],
                                                                      op0=ALU.mult, op1=ALU.add),
                         [("F", zs), *ck("lng", 4), *ck("lnb", 4)], [("F", zs)])
                    _stop("C4")
                    if m == 1:
                        _stop("C5")

                _stop("C")
                for hd in range(4):
                    transposes(lambda c, hd=hd: kT[:, hd, c * 128:(c + 1) * 128], 4,
                               lambda hd=hd: kp[:, :, hd * 128:(hd + 1) * 128],
                               [("kT", hd)], [("kp", hd)], "act")
                kvb = []
                for tt in range(4):
                    bk = bank(hold=True)

                    def fnk(h, tt=tt, bk=bk):
                        r = None
                        for hd in range(4):
                            r = h.matmul(ps[bk][:, hd * 128:(hd + 1) * 128], lhsT=kp[:, tt, hd * 128:(hd + 1) * 128],
                                         rhs=vtok[:, tt, hd * 128:(hd + 1) * 128], start=True, stop=True)
                        return r
                    P.op("pe", fnk, [("kp", hd) for hd in range(4)] + [("v", tt)], [("ps", bk)])
                    kvb.append(bk)
                def s_step(tt):
                    bk = kvb[tt]
                    for hd in range(4):
                        P.op("dve", lambda h, hd=hd, tt=tt: h.tensor_scalar(
                            out=Sst[:, hd, :], in0=Sst[:, hd, :], scalar1=dec[:, hd, tt:tt + 1], scalar2=None,
                            op0=ALU.mult),
                            [("S", hd), ("dec", hd)], [("S", hd)])
                        P.op("dve", lambda h, hd=hd, tt=tt, bk=bk: h.scalar_tensor_tensor(
                            out=Sst[:, hd, :], in0=ps[bk][:, hd * 128:(hd + 1) * 128], scalar=ebl4[:, hd, tt:tt + 1],
                            in1=Sst[:, hd, :], op0=ALU.mult, op1=ALU.add),
                            [("S", hd), ("ebl", hd), ("ps", bk)], [("S", hd)])
                    unhold(bk)

                def s_copy(tt):
                    P.op("act", lambda h, tt=tt: h.activation(out=Sbf[:, tt, :],
                                                              in_=Sst[:].rearrange("p a b -> p (a b)"), func=AF.Copy,
                                                              scale=float(math.exp(-MOFF))),
                         [("S", hd) for hd in range(4)], [("Sbf", tt)])

                obank = {}

                def attn_o(tt):
                    ba = bank()
                    ab = 0

                    def fna(h, tt=tt, ba=ba):
                        r = None
                        for hd in range(4):
                            r = h.matmul(ps[ba][:, hd * 128:(hd + 1) * 128],
                                         lhsT=kT[:, hd, tt * 128:(tt + 1) * 128],
                                         rhs=qT[:, hd, tt * 128:(tt + 1) * 128], start=True, stop=True)
                        return r
                    P.op("pe", fna, [("kT", hd) for hd in range(4)] + [("qT", hd) for hd in range(4)], [("ps", ba)])
                    P.op("dve", lambda h, ba=ba, ab=ab: h.tensor_tensor(
                        out=attnm[:, ab, :].rearrange("p (a b) -> p a b", b=128),
                        in0=ps[ba][:].rearrange("p (a b) -> p a b", b=128),
                        in1=_ap(maskT, 0, [[0, 4], [1, 128]]), op=ALU.mult),
                        [("ps", ba), ("c", "maskT")], [("attnm", ab)])
                def o_part(tt):
                    ab = 0
                    bo = bank(hold=True)
                    obank[tt] = bo
                    sp = (tt - 1) % 4

                    def fno(h, tt=tt, bo=bo, ab=ab, sp=sp):
                        r = None
                        for hd in range(4):
                            h.matmul(ps[bo][:, hd * 128:(hd + 1) * 128], lhsT=qT[:, hd, tt * 128:(tt + 1) * 128],
                                     rhs=Sbf[:, sp, hd * 128:(hd + 1) * 128], start=True, stop=False)
                            r = h.matmul(ps[bo][:, hd * 128:(hd + 1) * 128], lhsT=attnm[:, ab, hd * 128:(hd + 1) * 128],
                                         rhs=vtok[:, tt, hd * 128:(hd + 1) * 128], start=False, stop=True)
                        return r
                    P.op("pe", fno, [("qT", hd) for hd in range(4)] + [("Sbf", sp), ("attnm", ab), ("v", tt)],
                         [("ps", bo)])
                    for hd in range(4):
                        P.op("act", lambda h, hd=hd, tt=tt, bo=bo: h.activation(
                            out=sqj[:, hd * 128:(hd + 1) * 128], in_=ps[bo][:, hd * 128:(hd + 1) * 128], func=AF.Square,
                            accum_out=stat[:, 1, tt * 4 + hd:tt * 4 + hd + 1]),
                            [("ps", bo)], [("stat", 1, tt * 4 + hd), ("sqj",)])

                def o_evac(tt):
                    bo = obank[tt]
                    P.op("dve", lambda h, tt=tt, bo=bo: h.tensor_tensor(out=gsilu[:, tt, :], in0=ps[bo][:],
                                                                        in1=gsilu[:, tt, :], op=ALU.mult),
                         [("ps", bo), ("gsilu", tt)], [("gsilu", tt)])
                    unhold(bo)

                for tt in range(4):
                    s_step(tt)
                    if tt < 3:
                        s_copy(tt)
                attn_o(0)
                conv_chunk(0)
                o_part(0)
                s_copy(3)
                for tt in range(1, 4):
                    attn_o(tt)
                    conv_chunk(tt)
                    o_part(tt)
                    o_evac(tt - 1)
                o_evac(3)
                newton(stat[:, 1, 0:16], 16, 1.0 / 128, stat[:, 2, 0:16],
                       [("stat", 1, i) for i in range(16)], [("stat", 2, "r")])
                for m in range(4):
                    t2 = 4 + (m % 2) * 3
                    P.op("act", lambda h, m=m, t2=t2: h.activation(out=Fb[:, t2, :], in_=Fb[:, m, :], func=AF.Sigmoid),
                         [("F", m)], [("F", t2)])
                    P.op("pool", lambda h, m=m, t2=t2: h.tensor_tensor(out=xT[:, m, :], in0=Fb[:, m, :], in1=Fb[:, t2, :],
                                                                       op=ALU.mult),
                         [("F", m), ("F", t2)], [("xT", hb, m, tt) for tt in range(4)])
                for tt in range(4):
                    yb = 0
                    P.op("dve", lambda h, tt=tt, yb=yb: h.tensor_tensor(
                        out=yrec[:, yb, :].rearrange("p (a b) -> p a b", b=128),
                        in0=gsilu[:, tt, :].rearrange("p (a b) -> p a b", b=128),
                        in1=_ap(stat, 2 * 16 + tt * 4, [[1, 4], [0, 128]]), op=ALU.mult),
                        [("gsilu", tt), ("stat", 2, "r")], [("yrec", yb)])
                    transposes(lambda c, yb=yb: yrec[:, yb, c * 128:(c + 1) * 128], 4,
                               lambda tt=tt: xT[:, 4:8, tt * 128:(tt + 1) * 128],
                               [("yrec", yb)], [("xT", hb, 4 + c, tt) for c in range(4)], "act")

                _stop("D")
                s_o0 = wload(wout_b, 0, 8, 0)
                s_o1 = wload(wout_b, 0, 8, 512)

                def ffn_T(tt):
                    xb = tt % 2
                    transposes(lambda c, xb=xb: xn[:, xb, c * 128:(c + 1) * 128], 8,
                               lambda tt=tt: xT2[:, hb * 8:(hb + 1) * 8, tt * 128:(tt + 1) * 128],
                               [("xn", xb)], [("xT", hb, kc, tt) for kc in range(8)],
                               "act" if tt % 2 else "dve")

                for tt in range(4):
                    for c in range(2):
                        b = bank()
                        mm_tm(s_o0 if c == 0 else s_o1, tt, b)
                        P.op("dve", lambda h, tt=tt, b=b, c=c: h.tensor_tensor(
                            out=hbuf[:, hb, tt, c * 512:(c + 1) * 512], in0=ps[b][:],
                            in1=hbuf[:, hb, tt, c * 512:(c + 1) * 512], op=ALU.add),
                            [("ps", b), ("h", hb, tt)], [("h", hb, tt)])
                    P.op("act", lambda h, tt=tt: h.activation(out=sqj[:], in_=hbuf[:, hb, tt, :], func=AF.Square,
                                                              accum_out=stat[:, 3, tt:tt + 1]),
                         [("h", hb, tt)], [("stat", 3, tt), ("sqj",)])
                    newton(stat[:, 3, tt:tt + 1], 1, 1.0 / D, stat[:, 3, 4 + tt:5 + tt],
                           [("stat", 3, tt)], [("stat", 3, "r", tt)])
                    xb = tt % 2
                    P.op("pool" if tt % 2 == 0 else "dve", lambda h, tt=tt, xb=xb: h.tensor_scalar(
                        out=xn[:, xb, :], in0=hbuf[:, hb, tt, :], scalar1=stat[:, 3, 4 + tt:5 + tt],
                        scalar2=1.0, op0=ALU.mult, op1=ALU.mult),
                        [("h", hb, tt), ("stat", 3, "r", tt)], [("xn", xb)])
                    if tt >= 1:
                        ffn_T(tt - 1)
                ffn_T(3)

                _stop("E")
                s_p = wload(wple_b, 0, 2, 0, ncols=1024)
                pview = ring[:, s_p].rearrange("p k n -> p (k n)")[:, 0:2048].rearrange("p (k n) -> p k n", n=1024)
                for tt in range(4):
                    pbb = tt % 2
                    P.op("pool", lambda h, tt=tt, pbb=pbb: h.tensor_scalar(out=pb[:, pbb, :], in0=pin[:, tt, :],
                                                                          scalar1=1.0, scalar2=1.0, op0=ALU.mult,
                                                                          op1=ALU.mult),
                         [("pin", tt)], [("pb", pbb)])
                    transposes(lambda c, pbb=pbb: pb[:, pbb, c * 128:(c + 1) * 128], 2,
                               lambda tt=tt: pT[:, :, tt * 128:(tt + 1) * 128],
                               [("pb", pbb)], [("pT", tt)], "act")
                for tt in range(4):
                    for c in range(2):
                        b = bank()
                        mm_tm(s_p, tt, b, nkc=2, src=pT, ncols=512, c0=c * 512, slotview=pview)
                        P.op("dve", lambda h, b=b, c=c, tt=tt: h.tensor_copy(out=Fb[:, tt * 2 + c, :], in_=ps[b][:]),
                             [("ps", b)], [("F", tt * 2 + c)])
                        P.op("act", lambda h, b=b, c=c, tt=tt: h.activation(
                            out=sqj[:, 0:512], in_=ps[b][:], func=AF.Square,
                            accum_out=stat[:, 4, tt * 2 + c:tt * 2 + c + 1]),
                            [("ps", b)], [("stat", 4, tt * 2 + c), ("sqj",)])
                P.op("dve", lambda h: h.tensor_tensor(
                    out=stat[:, 5, 0:4], in0=_ap(stat, 4 * 16, [[2, 4]]), in1=_ap(stat, 4 * 16 + 1, [[2, 4]]),
                    op=ALU.add),
                    [("stat", 4, i) for i in range(8)], [("stat", 5, "s")])
                newton(stat[:, 5, 0:4], 4, 1.0 / D, stat[:, 5, 4:8], [("stat", 5, "s")], [("stat", 5, "r")])

                def gple_premult():
                    for tt in range(4):
                        for c in range(2):
                            P.op("pool", lambda h, c=c, tt=tt: h.tensor_tensor(
                                out=Fb[:, tt * 2 + c, :], in0=Fb[:, tt * 2 + c, :],
                                in1=gple_bc[:, c * 512:(c + 1) * 512], op=ALU.mult),
                                [("F", tt * 2 + c), ("c", "gple")], [("F", tt * 2 + c)])
                nxt = st + 1 < NST
                if nxt and st >= 1:
                    norm_stats(0, 1 - hb)
                for a in range(2):
                    for cbk in range(4):
                        s_u = wload(wup_b, 0, 8, a * 2048 + cbk * 512)
                        for m in range(4):
                            b = bank()
                            mm_fm(s_u, m, b)
                            ts = 8 + (m % 2)
                            ach = cbk * 4 + m
                            P.op("act", lambda h, b=b, ts=ts: h.activation(out=Fb[:, ts, :], in_=ps[b][:],
                                                                           func=AF.Square),
                                 [("ps", b)], [("F", ts)])
                            P.op("dve", lambda h, b=b, ts=ts, ach=ach: h.scalar_tensor_tensor(
                                out=aT[:, ach, :], in0=ps[b][:], scalar=0.0, in1=Fb[:, ts, :], op0=ALU.is_gt,
                                op1=ALU.mult),
                                [("ps", b), ("F", ts)], [("aT", ach)])
                        if a == 0 and cbk == 0:
                            gple_premult()
                    for c in range(2):
                        banks4 = [bank(hold=True) for _ in range(4)]
                        for rb in range(2):
                            s_d = wload(wdown_b, a * 2048 + rb * 1024, 8, c * 512)
                            for tt in range(4):
                                b = banks4[tt]

                                def fnd(h, tt=tt, b=b, rb=rb, s_d=s_d):
                                    r = None
                                    for kc in range(8):
                                        r = h.matmul(ps[b][:], lhsT=aT[:, rb * 8 + kc, tt * 128:(tt + 1) * 128],
                                                     rhs=ring[:, s_d, kc, :], start=(rb == 0 and kc == 0),
                                                     stop=(rb == 1 and kc == 7))
                                    return r
                                P.op("pe", fnd, rkeys(s_d) + [("aT", rb * 8 + kc) for kc in range(8)], [("ps", b)])
                        for tt in range(4):
                            b = banks4[tt]
                            P.op("dve", lambda h, tt=tt, b=b, c=c: h.tensor_tensor(
                                out=hbuf[:, hb, tt, c * 512:(c + 1) * 512], in0=ps[b][:],
                                in1=hbuf[:, hb, tt, c * 512:(c + 1) * 512], op=ALU.add),
                                [("ps", b), ("h", hb, tt)], [("h", hb, tt)])
                            unhold(b)

                _stop("F")
                s_g0 = wload(wgate_b, 0, 8, 0)
                s_g1 = wload(wgate_b, 0, 8, 512)
                if st == 0 and NST > 1:
                    load_x(1)
                    norm_stats(0, 1 - hb)
                if nxt:
                    norm_apply(0, 0, 1 - hb)

                def gate_tile(tt):
                    xb = tt % 2
                    P.op("pool", lambda h, tt=tt, xb=xb: h.tensor_scalar(out=xn[:, xb, :], in0=hbuf[:, hb, tt, :],
                                                                        scalar1=1.0, scalar2=1.0, op0=ALU.mult,
                                                                        op1=ALU.mult),
                         [("h", hb, tt)], [("xn", xb)])
                    transposes(lambda c, xb=xb: xn[:, xb, c * 128:(c + 1) * 128], 8,
                               lambda tt=tt: xT[:, :, tt * 128:(tt + 1) * 128],
                               [("xn", xb)], [("xT", hb, kc, tt) for kc in range(8)], "dve" if tt % 2 else "act")
                    for c in range(2):
                        b = bank()
                        mm_tm(s_g0 if c == 0 else s_g1, tt, b)
                        P.op("act", lambda h, b=b, c=c, tt=tt: h.activation(out=gsilu[:, (tt % 2) * 2 + c, :],
                                                                            in_=ps[b][:], func=AF.Sigmoid),
                             [("ps", b)], [("gsilu", (tt % 2) * 2 + c)])

                def combine_tile(tt):
                    for c in range(2):
                        gs = (tt % 2) * 2 + c
                        P.op("dve", lambda h, tt=tt, c=c, gs=gs: h.scalar_tensor_tensor(
                            out=Fb[:, tt * 2 + c, :], in0=Fb[:, tt * 2 + c, :], scalar=stat[:, 5, 4 + tt:5 + tt],
                            in1=gsilu[:, gs, :], op0=ALU.mult, op1=ALU.mult),
                            [("F", tt * 2 + c), ("gsilu", gs), ("stat", 5, "r")], [("F", tt * 2 + c)])
                        P.op("dve", lambda h, tt=tt, c=c: h.tensor_tensor(
                            out=hbuf[:, hb, tt, c * 512:(c + 1) * 512], in0=hbuf[:, hb, tt, c * 512:(c + 1) * 512],
                            in1=Fb[:, tt * 2 + c, :], op=ALU.add),
                            [("h", hb, tt), ("F", tt * 2 + c)], [("h", hb, tt)])
                    P.op("act", lambda h, tt=tt: h.activation(out=sqj[:], in_=hbuf[:, hb, tt, :], func=AF.Square,
                                                              accum_out=stat[:, 6, tt:tt + 1]),
                         [("h", hb, tt)], [("stat", 6, tt), ("sqj",)])

                for tt in range(4):
                    gate_tile(tt)
                    if tt >= 1:
                        combine_tile(tt - 1)
                if nxt:
                    first_pass[0] = False
                    prefetched[("w_in_b", 0, 1536)] = wload(win_b, 0, 8, 1536)
                    prefetched[("w_in_b", 0, 1024)] = wload(win_b, 0, 8, 1024)
                    prefetched[("w_in_b", 0, 512)] = wload(win_b, 0, 8, 512)
                combine_tile(3)
                _stop("G")
                newton(stat[:, 6, 0:4], 4, 1.0 / D, stat[:, 6, 4:8],
                       [("stat", 6, tt) for tt in range(4)], [("stat", 6, "r")])
                for tt in range(4):
                    P.op("dve", lambda h, tt=tt: h.scalar_tensor_tensor(
                        out=hbuf[:, hb, tt, :], in0=hbuf[:, hb, tt, :], scalar=stat[:, 6, 4 + tt:5 + tt],
                        in1=gfin_bc[:], op0=ALU.mult, op1=ALU.mult),
                        [("h", hb, tt), ("stat", 6, "r"), ("c", "gfin")], [("h", hb, tt)])
                    r0 = st * T + tt * 128
                    pending_stores.append((r0, hb, tt))
                if st + 1 >= NST:
                    deferred_stores()

            for st in range(NST):
                do_st(st)
        except _Stop:
            pass
        P.emit()
    return nc


_NC_CACHE = {}


def kernel(x, p, norm_mix_g, w_in, conv_w, conv_b, conv_ln_g, conv_ln_b, lb_logits, rec_norm_g, w_out,
           norm_ffn_g, w_up, w_down, w_ple, ple_norm_g, w_ple_gate, final_norm_g):
    f = lambda a: np.ascontiguousarray(np.asarray(a, dtype=np.float32))
    x = f(x)
    p = f(p)
    B, S, _ = x.shape
    if S not in _NC_CACHE:
        _NC_CACHE[S] = build(S)
    nc = _NC_CACHE[S]
    shared = {
        "norm_mix_g": f(norm_mix_g)[0], "w_in": f(w_in)[0], "conv_w": f(conv_w)[0], "conv_b": f(conv_b)[0],
        "conv_ln_g": f(conv_ln_g)[0], "conv_ln_b": f(conv_ln_b)[0], "lb_logits": f(lb_logits),
        "rec_norm_g": f(rec_norm_g)[0], "w_out": f(w_out)[0], "norm_ffn_g": f(norm_ffn_g)[0],
        "w_up": f(w_up)[0], "w_down": f(w_down)[0], "w_ple": f(w_ple)[0], "ple_norm_g": f(ple_norm_g)[0],
        "w_ple_gate": f(w_ple_gate)[0], "final_norm_g": f(final_norm_g),
    }
    in_maps = []
    for b in range(B):
        m = dict(shared)
        m["x"] = x[b]
        m["p"] = p[0, b]
        in_maps.append(m)
    res = run_bass_kernel_spmd(nc, in_maps, core_ids=list(range(B)))
    return np.stack([np.asarray(r["out"], dtype=np.float32) for r in res.results], axis=0)
```

```python
import math
from contextlib import ExitStack

import numpy as np
import concourse.bass as bass
import concourse.mybir as mybir
from concourse.bass_utils import run_bass_kernel_spmd

dt = mybir.dt
F32 = dt.float32
BF16 = dt.bfloat16
I32 = dt.int32
ALU = mybir.AluOpType
AF = mybir.ActivationFunctionType

D = 1024
DIN = 3072
DC = 512
DR = 512
NH = 4
DFF = 4096
PLE = 256
CW = 31
T = 512
EPS = 1e-6
MOFF = 44.0
NB = 3
SEQ = 4096
NCORES = 8


BLOCKS = []
for _c0 in (1536, 1024, 512, 0, 2560, 2048):
    BLOCKS.append(("w_in_b", "w_in", 0, _c0, 8, 512))
for _c0 in (0, 512):
    BLOCKS.append(("w_out_b", "w_out", 0, _c0, 8, 512))
BLOCKS.append(("w_ple_b", "w_ple", 0, 0, 2, 1024))
for _a in range(2):
    for _cb in range(4):
        BLOCKS.append(("w_up_b", "w_up", 0, _a * 2048 + _cb * 512, 8, 512))
    for _c in range(2):
        for _rb in range(2):
            BLOCKS.append(("w_down_b", "w_down", _a * 2048 + _rb * 1024, _c * 512, 8, 512))
for _c0 in (0, 512):
    BLOCKS.append(("w_gate_b", "w_ple_gate", 0, _c0, 8, 512))
NBLK = len(BLOCKS)
BLK_IDX = {(b[0], b[2], b[3]): i for i, b in enumerate(BLOCKS)}


def pack_weights(ws):
    out = np.zeros((NBLK, 128, 4096), np.float32)
    for i, (_, fname, r0, c0, nkc, ncols) in enumerate(BLOCKS):
        w = ws[fname][r0:r0 + nkc * 128, c0:c0 + ncols]
        out[i, :, :nkc * ncols] = w.reshape(nkc, 128, ncols).transpose(1, 0, 2).reshape(128, nkc * ncols)
    return out


class _Op:
    __slots__ = ("eng", "fn", "deps", "signal", "idx", "chan", "chan_val", "wait_total")


class Prog:
    ENG = ["pe", "act", "dve", "pool", "sp"]

    def __init__(self, nc, es):
        self.nc = nc
        self.es = es
        self.ops = {e: [] for e in self.ENG}
        self.sems = {e: es.enter_context(nc.semaphore("sem_" + e)) for e in self.ENG}
        self.chans = {}
        self.last_w = {}
        self.readers = {}
        self.touch = {}
        self.nops = 0

    def op(self, eng, fn, reads=(), writes=(), chan=None, wait_total=False):
        o = _Op()
        o.eng = eng
        o.fn = fn
        o.deps = []
        o.signal = False
        o.idx = 0
        o.chan = chan
        o.chan_val = 0
        o.wait_total = wait_total
        if chan is not None:
            c = self.chans.get(chan)
            if c is None:
                c = self.chans[chan] = [
                    self.es.enter_context(self.nc.semaphore("ch%d" % len(self.chans))), 0]
            c[1] += 16
            o.chan_val = c[1]
        writes = list(writes) + [k for k in reads if k[0] == "ps"]
        reads = [k for k in reads if k[0] != "ps"]
        self.nops += 1
        for k in writes:
            if k[0] == "ps":
                self.touch[k[1]] = self.nops
        deps = {}
        for k in reads:
            w = self.last_w.get(k)
            if w is not None:
                deps.setdefault(id(w), [w, set()])[1].add("raw")
        for k in writes:
            w = self.last_w.get(k)
            if w is not None:
                deps.setdefault(id(w), [w, set()])[1].add("waw")
            for r in self.readers.get(k, ()):
                deps.setdefault(id(r), [r, set()])[1].add("war")
        for d, kinds in deps.values():
            if d is o:
                continue
            if d.chan is None and d.eng == eng:
                if eng == "pe":
                    continue
            o.deps.append(d)
            if d.chan is None:
                d.signal = True
        for k in reads:
            self.readers.setdefault(k, []).append(o)
        for k in writes:
            self.last_w[k] = o
            self.readers[k] = []
        self.ops[eng].append(o)
        return o

    def emit(self):
        for e in self.ENG:
            n = 0
            for o in self.ops[e]:
                if o.chan is None and o.signal:
                    n += 1
                    o.idx = n
        handles = None
        with self.nc.Block() as blk:
            starters = {"pe": blk.tensor, "act": blk.scalar, "dve": blk.vector,
                        "pool": blk.gpsimd, "sp": blk.sync}
            for e in self.ENG:
                def body(h, e=e):
                    known = {}
                    for o in self.ops[e]:
                        for d in o.deps:
                            if d.chan is not None:
                                c = self.chans[d.chan]
                                sem, val = c[0], (c[1] if d.wait_total else d.chan_val)
                            else:
                                sem, val = self.sems[d.eng], d.idx
                            key = sem.num
                            if known.get(key, 0) < val:
                                h.wait_ge(sem, val)
                                known[key] = val
                        ins = o.fn(h)
                        if o.chan is not None:
                            ins.then_inc(self.chans[o.chan][0], 16)
                        elif o.signal:
                            ins.then_inc(self.sems[e], 1)
                    if e == "sp":
                        for name, c in self.chans.items():
                            if known.get(c[0].num, 0) < c[1]:
                                h.wait_ge(c[0], c[1])
                starters[e](body)


class _Stop(Exception):
    pass


STOP = None


def _stop(name):
    if STOP == name:
        raise _Stop()


def _ap(t, off, dims):
    pstep = 1
    for s in list(t.shape)[1:]:
        pstep *= int(s)
    return bass.AP(t, off, [[pstep, 128]] + [list(d) for d in dims])


def build(S):
    NST = S // T
    NT = S // 128
    nc = bass.Bass("TRN2", target_bir_lowering=False)

    def din(name, shape):
        return nc.dram_tensor(name, shape, F32, kind="ExternalInput").ap()

    x_d = din("x", [S, D])
    p_d = din("p", [S, PLE])
    gmix_d = din("norm_mix_g", [D])
    convw_d = din("conv_w", [CW, DC])
    convb_d = din("conv_b", [DC])
    lng_d = din("conv_ln_g", [DC])
    lnb_d = din("conv_ln_b", [DC])
    lbl_d = din("lb_logits", [2, DR])
    grec_d = din("rec_norm_g", [DR])
    gffn_d = din("norm_ffn_g", [D])
    gple_d = din("ple_norm_g", [D])
    gfin_d = din("final_norm_g", [D])
    wf32 = din("wf32", [NBLK, 128, 4096])
    out_d = nc.dram_tensor("out", [S, D], F32, kind="ExternalOutput").ap()

    def dscr(name, shape):
        return nc.dram_tensor(name, shape, BF16, kind="Internal").ap()

    win_b = dscr("w_in_b", [D, DIN])
    wout_b = dscr("w_out_b", [D, D])
    wup_b = dscr("w_up_b", [D, DFF])
    wdown_b = dscr("w_down_b", [DFF, D])
    wple_b = dscr("w_ple_b", [PLE, D])
    wgate_b = dscr("w_gate_b", [D, D])
    wscr = dscr("wscr", [NBLK, 128, 4096])

    def blk(name, r0, c0):
        return BLK_IDX[(name, r0, c0)]

    es = ExitStack()
    with es:
        def sb(name, shape, dtype):
            return es.enter_context(nc.sbuf_tensor(name, shape, dtype))

        ring = sb("ring", [128, NB, 8, 512], BF16)
        big = sb("big", [128, 4096], F32)
        hbuf = sb("hbuf", [128, 2, 4, 1024], F32)
        pin = sb("pin", [128, 4, 256], F32)
        xn = sb("xn", [128, 2, 1024], BF16)
        xT2 = sb("xT", [128, 16, 512], BF16)
        sqj = sb("sqj", [128, 1024], BF16)
        Fb = sb("Fb", [128, 10, 512], F32)
        gsilu = sb("gsilu", [128, 4, 512], F32)
        cv32 = sb("cv32", [128, 2, 512], F32)
        cvb = sb("cvb", [128, 2, 512], BF16)
        csq = sb("csq", [128, 2, 512], BF16)
        cT = sb("cT", [128, 4, 544], BF16)
        qT = sb("qT", [128, 4, 512], BF16)
        kT = sb("kT", [128, 4, 512], BF16)
        vtok = sb("vtok", [128, 4, 512], BF16)
        kp = sb("kp", [128, 4, 512], BF16)
        attnm = sb("attnm", [128, 1, 512], BF16)
        yrec = sb("yrec", [128, 1, 512], BF16)
        Sst = sb("Sst", [128, 4, 128], F32)
        Sbf = sb("Sbf", [128, 4, 512], BF16)
        ebl4 = sb("ebl4", [128, 4, 4], F32)
        pb = sb("pb", [128, 2, 256], BF16)
        pT = sb("pT", [128, 2, 512], BF16)
        diag = sb("diag", [128, CW * 4, 128], BF16)
        ident = sb("ident", [128, 128], BF16)
        ones32 = sb("ones32", [128, 128], F32)
        maskT = sb("maskT", [128, 128], F32)
        blkm = sb("blkm", [128, 128], BF16)
        gple_bc = sb("gple_bc", [128, 1024], F32)
        gfin_bc = sb("gfin_bc", [128, 1024], F32)
        gmix = sb("gmix", [128, 8], F32)
        gout = sb("gout", [128, 8], F32)
        lbv = sb("lbv", [128, 4], F32)
        omlb = sb("omlb", [128, 4], F32)
        vstA = gsilu[:, 0, 0:128]
        cwst = gsilu[:, 0, 128:256]
        vecs = sb("vecs", [128, 32], F32)
        cwt = sb("cwt", [128, CW * 4], F32)
        ident32 = gsilu[:, 0, 256:384]
        gffn = vecs[:, 0:8]
        convb = vecs[:, 12:16]
        lng = vecs[:, 16:20]
        lnb = vecs[:, 20:24]
        lbl = vecs[:, 24:32].rearrange("p (r c) -> p r c", c=4)
        cw = cwt[:].rearrange("p (k m) -> p k m", m=4)
        dec = sb("dec", [128, 4, 4], F32)
        stat = sb("stat", [128, 8, 16], F32)
        nt = sb("nt", [128, 4, 16], F32)

        ps = [es.enter_context(nc.psum_tensor("ps%d" % i, [128, 512], F32)) for i in range(8)]
        P = Prog(nc, es)

        aT = big[:].bitcast(BF16).rearrange("p (a b) -> p a b", b=512)
        stage0 = big[:].rearrange("p (a b) -> p a b", b=512)
        stage1 = Fb[:, 0:8, :]

        held = set()

        def bank(hold=False):
            cands = [b for b in range(8) if b not in held]
            b = min(cands, key=lambda b: P.touch.get(b, -b))
            P.nops += 1
            P.touch[b] = P.nops
            if hold:
                held.add(b)
            return b

        def unhold(b):
            held.discard(b)

        def dma(out, in_, reads, writes, chan, wait_total=False, slow=False):
            if slow:
                fn = lambda h: h.dma_start(out=out, in_=in_, allow_slow_non_contiguous=True)
            else:
                fn = lambda h: h.dma_start(out=out, in_=in_)
            return P.op("sp", fn, reads, writes, chan=chan, wait_total=wait_total)

        def vec_load(tile, src, ncol):
            for c in range(ncol):
                dma(tile[:, c:c + 1], src[c * 128:(c + 1) * 128].rearrange("(p o) -> p o", o=1),
                    [], [("c", tile.name, c)], chan="misc", wait_total=True)

        def ck(name, n):
            return [("c", name, c) for c in range(n)]

        def newton(src_ap, n, scale, dst_ap, rkeys, wkeys):
            tx = nt[:, 0, 0:n]
            P.op("act", lambda h: h.activation(out=tx, in_=src_ap, func=AF.Ln, scale=float(scale), bias=float(EPS)),
                 rkeys, [("nt", 0)])
            P.op("act", lambda h: h.activation(out=dst_ap, in_=tx, func=AF.Exp, scale=-0.5), [("nt", 0)], wkeys)

        try:
            P.op("pool", lambda h: h.memset(ones32[:], 1.0), [], [("c", "ones32")])
            P.op("pool", lambda h: h.affine_select(out=maskT[:], in_=ones32[:], pattern=[[1, 128]],
                                                   compare_op=ALU.is_ge, fill=0.0, base=0, channel_multiplier=-1),
                 [("c", "ones32")], [("c", "maskT")])
            P.op("pool", lambda h: h.affine_select(out=ident[:], in_=ones32[:], pattern=[[1, 128]],
                                                   compare_op=ALU.is_equal, fill=0.0, base=0, channel_multiplier=-1),
                 [("c", "ones32")], [("c", "ident")])
            P.op("pool", lambda h: h.affine_select(out=ident32[:], in_=ones32[:], pattern=[[1, 128]],
                                                   compare_op=ALU.is_equal, fill=0.0, base=0, channel_multiplier=-1),
                 [("c", "ones32")], [("c", "ident32")])
            P.op("pool", lambda h: h.memset(blkm[:], 0.0), [], [("c", "blkm")])
            P.op("pool", lambda h: h.memset(blkm[0:64, 0:64], 1.0 / 64), [("c", "blkm")], [("c", "blkm")])
            P.op("pool", lambda h: h.memset(blkm[64:128, 64:128], 1.0 / 64), [("c", "blkm")], [("c", "blkm")])
            P.op("pool", lambda h: h.memset(cT[:], 0.0), [], [("cT", m) for m in range(4)])
            P.op("pool", lambda h: h.memset(Sst[:], 0.0), [], [("S", hd) for hd in range(4)])
            P.op("pool", lambda h: h.memset(Sbf[:], 0.0), [], [("Sbf", t_) for t_ in range(4)])
            P.op("pool", lambda h: h.memset(gout[:], 1.0), [], ck("gout", 8))

            _stop("consts")
            for c in range(8):
                dma(gmix[:, c:c + 1], gmix_d[c * 128:(c + 1) * 128].rearrange("(p o) -> p o", o=1),
                    [], [("c", "gmix", c)], chan="misc0", wait_total=True)

            def late_consts():
                rows = [(gffn_d, 0, 8, None), (grec_d, 8, 4, None), (convb_d, 12, 4, None), (lng_d, 16, 4, None),
                        (lnb_d, 20, 4, None), (lbl_d, 24, 8, "2d")]
                for src, r0, n, kind in rows:
                    if kind == "2d":
                        src_ap = src.rearrange("r (c p) -> (r c) p", p=128)
                    else:
                        src_ap = src.rearrange("(c p) -> c p", p=128)
                    dma(vstA[r0:r0 + n, :], src_ap, [], [("gsilu", 0, "v", r0)], chan="misc", wait_total=True)
                dma(cwst[0:CW * 4, :], convw_d.rearrange("k (m p) -> (k m) p", p=128), [], [("gsilu", 0, "c")],
                    chan="misc", wait_total=True)
                dma(gple_bc[:], bass.AP(gple_d.tensor, 0, [[0, 128], [1, 1024]]), [], [("c", "gple")],
                    chan="misc", wait_total=True)
                dma(gfin_bc[:], bass.AP(gfin_d.tensor, 0, [[0, 128], [1, 1024]]), [], [("c", "gfin")],
                    chan="misc", wait_total=True)
                bv = bank()
                P.op("pe", lambda h: h.transpose(out=ps[bv][:, 0:32], in_=vstA[0:32, :], identity=ident32[0:32, 0:32]),
                     [("gsilu", 0, "v", r0) for _, r0, _, _ in rows] + [("c", "ident32")], [("ps", bv)])
                vkeys = ck("gffn", 8) + ck("convb", 4) + ck("lng", 4) + ck("lnb", 4) + ck("lbl", 8) + [("c", "grec")]
                P.op("dve", lambda h: h.tensor_copy(out=vecs[:], in_=ps[bv][:, 0:32]), [("ps", bv)], vkeys)
                P.op("dve", lambda h: h.tensor_copy(out=gout[:, 4:8], in_=vecs[:, 8:12]), [("c", "grec")],
                     [("c", "gout", 4 + c) for c in range(4)])
                bc = bank()
                P.op("pe", lambda h: h.transpose(out=ps[bc][:, 0:CW * 4], in_=cwst[0:CW * 4, :],
                                                 identity=ident32[0:CW * 4, 0:CW * 4]),
                     [("gsilu", 0, "c"), ("c", "ident32")], [("ps", bc)])
                P.op("dve", lambda h: h.tensor_copy(out=cwt[:], in_=ps[bc][:, 0:CW * 4]), [("ps", bc)],
                     [("c", "cw", k) for k in range(CW)])
                P.op("dve", lambda h: h.tensor_tensor(out=omlb[:], in0=lbl[:, 0, :], in1=lbl[:, 1, :],
                                                      op=ALU.subtract),
                     ck("lbl", 8), [("c", "omlb")])
                P.op("act", lambda h: h.activation(out=lbv[:], in_=omlb[:], func=AF.Sigmoid),
                     [("c", "omlb")], [("c", "lbv")])
                P.op("dve", lambda h: h.tensor_scalar(out=omlb[:], in0=lbv[:], scalar1=-1.0, scalar2=1.0,
                                                      op0=ALU.mult, op1=ALU.add),
                     [("c", "lbv")], [("c", "omlb")])
            def diag_consts():
                for k in range(CW):
                    for m in range(4):
                        P.op("pool", lambda h, k=k, m=m: h.tensor_scalar(
                            out=diag[:, k * 4 + m, :], in0=ident[:], scalar1=cw[:, k, m:m + 1], scalar2=1.0,
                            op0=ALU.mult, op1=ALU.mult),
                            [("c", "ident"), ("c", "cw", k)], [("c", "diag")])

            late_consts()
            fmap = {"w_in_b": (None, gmix, "gmix"), "w_out_b": (None, gout, "gout"),
                    "w_up_b": (None, gffn, "gffn"), "w_down_b": (None, None, None),
                    "w_ple_b": (None, None, None), "w_gate_b": (None, None, None)}
            first_pass = [True]
            cast_i = [0]

            def cast_op(o_ap, i_ap, sc, rds, wrs):
                eng = ("act", "dve")[cast_i[0] % 2]
                cast_i[0] += 1
                if eng == "act":
                    if sc is not None:
                        fn = lambda h: h.activation(out=o_ap, in_=i_ap, func=AF.Copy, scale=sc)
                    else:
                        fn = lambda h: h.activation(out=o_ap, in_=i_ap, func=AF.Copy)
                else:
                    s1 = sc if sc is not None else 1.0
                    fn = lambda h: h.tensor_scalar(out=o_ap, in0=i_ap, scalar1=s1, scalar2=1.0,
                                                   op0=ALU.mult, op1=ALU.mult)
                P.op(eng, fn, rds, wrs)

            def wload_first(src, r0, nkc, c0, ncols, slot):
                fsrc, sct, sck = fmap[src.tensor.name]
                rkey = ("ring", slot)
                if ncols == 512:
                    for half in range(2):
                        stg = hbuf[:, 1, 2 * half:2 * half + 2, :].rearrange("p a (b n) -> p (a b) n", n=512)
                        hk = [("h", 1, 2 * half), ("h", 1, 2 * half + 1)]
                        dma(stg.rearrange("p k n -> p (k n)"),
                            wf32[blk(src.tensor.name, r0, c0)][:, half * 2048:(half + 1) * 2048],
                            [], hk, chan=("stg", half))
                        for k4 in range(4):
                            kc = half * 4 + k4
                            gi = r0 // 128 + kc
                            sc = sct[:, gi:gi + 1] if sct is not None else None
                            rds = hk + ([("c", sck, gi)] if sct is not None else [])
                            cast_op(ring[:, slot, kc, :], stg[:, k4, :], sc, rds, [(rkey, kc)])
                    dma(wscr[blk(src.tensor.name, r0, c0)], ring[:, slot].rearrange("p k n -> p (k n)"),
                        [(rkey, kc) for kc in range(8)], [("scr", src.tensor.name, r0, c0)], chan=("ring", slot))
                else:
                    stg = hbuf[:, 1, 0:2, :]
                    hk = [("h", 1, 0), ("h", 1, 1)]
                    dma(stg.rearrange("p k n -> p (k n)"), wf32[blk(src.tensor.name, 0, 0)][:, 0:2048], [], hk,
                        chan=("stg", 0))
                    pv = ring[:, slot].rearrange("p k n -> p (k n)")[:, 0:2048].rearrange("p (k n) -> p k n", n=1024)
                    for kc in range(2):
                        cast_op(pv[:, kc, :], stg[:, kc, :], None, hk, [(rkey, kc)])
                    dma(wscr[blk(src.tensor.name, 0, 0)][:, 0:2048],
                        ring[:, slot].rearrange("p k n -> p (k n)")[:, 0:2048],
                        [(rkey, kc) for kc in range(2)],
                        [("scr", src.tensor.name, 0, 0), ("scr", src.tensor.name, 0, 512)], chan=("ring", slot))

            ring_i = [0]
            prefetched = {}
            pending_stores = []

            def deferred_stores():
                while pending_stores:
                    r0, hb_, tt_ = pending_stores.pop(0)
                    dma(out_d[r0:r0 + 128, :], hbuf[:, hb_, tt_, :], [("h", hb_, tt_)], [], chan=("o", hb_, tt_))

            def wload(src, r0, nkc, c0, ncols=512):
                pk = (src.tensor.name, r0, c0)
                if pk in prefetched:
                    return prefetched.pop(pk)
                i = ring_i[0]
                ring_i[0] += 1
                slot = i % NB
                rkey = ("ring", slot)
                if first_pass[0]:
                    wload_first(src, r0, nkc, c0, ncols, slot)
                    return slot
                nel = nkc * ncols
                src_ap = wscr[blk(src.tensor.name, r0, c0)][:, 0:nel]
                dst_ap = ring[:, slot].rearrange("p k n -> p (k n)")[:, 0:nel]
                dma(dst_ap, src_ap, [("scr", src.tensor.name, r0, c0 + j * 512) for j in range(ncols // 512)],
                    [(rkey, kc) for kc in range(8)], chan=("ring", slot))
                return slot

            def rkeys(slot, nkc=8):
                return [(("ring", slot), kc) for kc in range(nkc)]

            def transposes(src_fn, nblk, dst_fn, src_keys, dst_keys, evac_eng):
                b = bank()
                pv = ps[b][:].bitcast(BF16)

                def fn(h):
                    r = None
                    for c in range(nblk):
                        r = h.transpose(out=pv[:, c * 128:(c + 1) * 128], in_=src_fn(c), identity=ident[:])
                    return r
                P.op("pe", fn, src_keys + [("c", "ident")], [("ps", b)])
                src3 = pv[:, 0:nblk * 128].rearrange("p (c t) -> p c t", t=128)
                if evac_eng == "act":
                    P.op("act", lambda h: h.activation(out=dst_fn(), in_=src3, func=AF.Copy), [("ps", b)], dst_keys)
                else:
                    P.op("dve", lambda h: h.tensor_copy(out=dst_fn(), in_=src3), [("ps", b)], dst_keys)

            def load_x(st):
                hb = st % 2
                for tt in range(4):
                    r0 = st * T + tt * 128
                    dma(hbuf[:, hb, tt, :], x_d[r0:r0 + 128, :], [], [("h", hb, tt)], chan=("x", hb, tt))

            load_x(0)

            def do_st(st):
                hb = st % 2
                first_pass[0] = (st == 0)
                for tt in range(4):
                    r0 = st * T + tt * 128
                    dma(pin[:, tt, :], p_d[r0:r0 + 128, :], [], [("pin", tt)], chan=("p", tt))

                def norm_stats(statrow, hbx):
                    for tt in range(4):
                        P.op("act", lambda h, tt=tt: h.activation(out=sqj[:], in_=hbuf[:, hbx, tt, :], func=AF.Square,
                                                                  accum_out=stat[:, statrow, tt:tt + 1]),
                             [("h", hbx, tt)], [("stat", statrow, tt), ("sqj",)])
                    newton(stat[:, statrow, 0:4], 4, 1.0 / D, stat[:, statrow, 4:8],
                           [("stat", statrow, tt) for tt in range(4)], [("stat", statrow, "r")])

                def norm_apply(statrow, evac_alt, hbx):
                    for tt in range(4):
                        xb = tt % 2
                        P.op("pool", lambda h, tt=tt, xb=xb: h.tensor_scalar(
                            out=xn[:, xb, :], in0=hbuf[:, hbx, tt, :], scalar1=stat[:, statrow, 4 + tt:5 + tt],
                            scalar2=1.0, op0=ALU.mult, op1=ALU.mult),
                            [("h", hbx, tt), ("stat", statrow, "r")], [("xn", xb)])
                        transposes(lambda c, xb=xb: xn[:, xb, c * 128:(c + 1) * 128], 8,
                                   lambda tt=tt: xT2[:, hbx * 8:(hbx + 1) * 8, tt * 128:(tt + 1) * 128],
                                   [("xn", xb)], [("xT", hbx, kc, tt) for kc in range(8)],
                                   "act" if (tt + evac_alt) % 2 == 0 else "dve")

                def norm_to_xT(statrow, evac_alt):
                    norm_stats(statrow, hb)
                    norm_apply(statrow, evac_alt, hb)

                if st == 0:
                    norm_to_xT(0, 0)
                _stop("A")
                xT = xT2[:, hb * 8:(hb + 1) * 8, :]
                xT_all = [("xT", hb, kc, tt) for kc in range(8) for tt in range(4)]

                def mm_fm(slot, m, b, nkc=8):
                    def fn(h):
                        r = None
                        for kc in range(nkc):
                            r = h.matmul(ps[b][:], lhsT=ring[:, slot, kc, m * 128:(m + 1) * 128], rhs=xT[:, kc, :],
                                         start=(kc == 0), stop=(kc == nkc - 1))
                        return r
                    P.op("pe", fn, rkeys(slot) + xT_all, [("ps", b)])

                def mm_tm(slot, tt, b, nkc=8, src=None, ncols=512, c0=0, slotview=None):
                    srcT = xT if src is None else src

                    def fn(h):
                        r = None
                        for kc in range(nkc):
                            rhs = ring[:, slot, kc, :] if slotview is None else slotview[:, kc, c0:c0 + ncols]
                            r = h.matmul(ps[b][:], lhsT=srcT[:, kc, tt * 128:(tt + 1) * 128], rhs=rhs,
                                         start=(kc == 0), stop=(kc == nkc - 1))
                        return r
                    keys = [("xT", hb, kc, tt) for kc in range(nkc)] if src is None else [("pT", tt)]
                    P.op("pe", fn, rkeys(slot) + keys, [("ps", b)])

                s_f = wload(win_b, 0, 8, 1536)
                s_q = wload(win_b, 0, 8, 1024)
                for hd in range(4):
                    b = bank()
                    mm_fm(s_f, hd, b)
                    P.op("act", lambda h, hd=hd, b=b: h.activation(out=Fb[:, hd, :], in_=ps[b][:], func=AF.Sigmoid),
                         [("ps", b)], [("F", hd)])
                    P.op("pool", lambda h, hd=hd: h.tensor_scalar(out=Fb[:, hd, :], in0=Fb[:, hd, :],
                                                                 scalar1=omlb[:, hd:hd + 1], scalar2=lbv[:, hd:hd + 1],
                                                                 op0=ALU.mult, op1=ALU.add),
                         [("F", hd), ("c", "omlb"), ("c", "lbv")], [("F", hd)])
                qbanks = []
                for hd in range(4):
                    b = bank(hold=True)
                    mm_fm(s_q, hd, b)
                    qbanks.append(b)
                s_cb = wload(win_b, 0, 8, 512)
                s_ca = wload(win_b, 0, 8, 0)

                def conv_ba(m):
                    sg = 5 + 3 * (m % 2)
                    b = bank()
                    mm_fm(s_cb, m, b)
                    P.op("act", lambda h, b=b, sg=sg: h.activation(out=Fb[:, sg, :], in_=ps[b][:], func=AF.Sigmoid),
                         [("ps", b)], [("F", sg)])
                    b2 = bank()
                    mm_fm(s_ca, m, b2)
                    P.op("dve", lambda h, m=m, b2=b2, sg=sg: h.tensor_tensor(out=cT[:, m, 30:542], in0=ps[b2][:],
                                                                             in1=Fb[:, sg, :], op=ALU.mult),
                         [("ps", b2), ("F", sg)], [("cT", m)])

                def g_tile(tt, s_g):
                    b = bank()
                    mm_tm(s_g, tt, b)
                    xk = []
                    if st == 0 and tt == 0:
                        xk = [("gsilu", 0, "v", r0_) for r0_ in (0, 8, 12, 16, 20, 24)] + [("gsilu", 0, "c"),
                                                                                              ("c", "ident32")]
                    P.op("act", lambda h, tt=tt, b=b: h.activation(out=gsilu[:, tt, :], in_=ps[b][:], func=AF.Sigmoid),
                         [("ps", b)], [("gsilu", tt)] + xk)
                    P.op("dve", lambda h, tt=tt, b=b: h.tensor_tensor(out=gsilu[:, tt, :], in0=ps[b][:],
                                                                      in1=gsilu[:, tt, :], op=ALU.mult),
                         [("ps", b), ("gsilu", tt)], [("gsilu", tt)])

                def i_tile(tt, s_i):
                    b = bank()
                    mm_tm(s_i, tt, b)
                    P.op("act", lambda h, tt=tt, b=b: h.activation(out=vtok[:, tt, :], in_=ps[b][:], func=AF.Copy),
                         [("ps", b)], [("v", tt)])

                def chain(hd):
                        base = 4 + (hd % 2) * 3
                        t1, t2, t3 = base, base + 1, base + 2
                        for tt in range(4):
                            P.op("dve", lambda h, hd=hd, tt=tt, t1=t1: h.tensor_tensor_scan(
                                out=Fb[:, t1, tt * 128:(tt + 1) * 128], data0=Fb[:, hd, tt * 128:(tt + 1) * 128],
                                data1=ones32[:], initial=float(math.exp(MOFF)), op0=ALU.mult, op1=ALU.mult),
                                [("F", hd), ("c", "ones32")], [("F", t1)])
                        P.op("dve", lambda h, t1=t1, t3=t3: h.reciprocal(out=Fb[:, t3, :], in_=Fb[:, t1, :]),
                             [("F", t1)], [("F", t3)])
                        P.op("act", lambda h, hd=hd, t1=t1: h.activation(
                            out=dec[:, hd, :], in_=_ap(Fb, t1 * 512 + 127, [[128, 4]]), func=AF.Copy,
                            scale=float(math.exp(-MOFF))),
                            [("F", t1)], [("dec", hd)])
                        P.op("act", lambda h, hd=hd, t1=t1: h.activation(
                            out=ebl4[:, hd, :], in_=_ap(Fb, t1 * 512 + 127, [[128, 4]]), func=AF.Copy),
                            [("F", t1)], [("ebl", hd)])
                        P.op("dve", lambda h, hd=hd, t3=t3: h.scalar_tensor_tensor(
                            out=Fb[:, t3, :], in0=Fb[:, hd, :], scalar=1.0, in1=Fb[:, t3, :], op0=ALU.subtract,
                            op1=ALU.mult),
                            [("F", hd), ("F", t3)], [("F", t3)])
                        P.op("act", lambda h, hd=hd, t3=t3: h.activation(out=kT[:, hd, :], in_=Fb[:, t3, :], func=AF.Copy,
                                                                         scale=-1.0),
                             [("F", t3)], [("kT", hd)])
                        qb = qbanks[hd]
                        P.op("dve", lambda h, hd=hd, qb=qb, t1=t1: h.tensor_tensor(out=qT[:, hd, :], in0=ps[qb][:],
                                                                                  in1=Fb[:, t1, :], op=ALU.mult),
                             [("ps", qb), ("F", t1)], [("qT", hd)])
                        unhold(qb)


                chain(0)
                conv_ba(0)
                conv_ba(1)
                chain(1)
                conv_ba(2)
                conv_ba(3)
                if st == 0:
                    diag_consts()
                s_g = wload(win_b, 0, 8, 2560)
                chain(2)
                for tt in range(4):
                    g_tile(tt, s_g)
                s_i = wload(win_b, 0, 8, 2048)
                deferred_stores()
                if st + 1 < NST and st != 0:
                    load_x(st + 1)
                chain(3)
                for tt in range(4):
                    i_tile(tt, s_i)

                _stop("B4")
                def conv_chunk(m):
                    b = bank()

                    def fnc(h, m=m, b=b):
                        r = None
                        for k in range(CW):
                            r = h.matmul(ps[b][:], lhsT=diag[:, k * 4 + m, :], rhs=cT[:, m, k:k + 512],
                                         start=(k == 0), stop=(k == CW - 1))
                        return r
                    P.op("pe", fnc, [("cT", m), ("c", "diag")], [("ps", b)])
                    P.op("pool", lambda h, m=m: h.tensor_copy(out=cT[:, m, 0:30], in_=cT[:, m, 512:542]),
                         [("cT", m)], [("cT", m)])
                    _stop("C1")
                    if m == 1:
                        _stop("C1b")
                    cb2 = m % 2
                    P.op("act", lambda h, m=m, b=b, cb2=cb2: h.activation(out=cvb[:, cb2, :], in_=ps[b][:],
                                                                          func=AF.Identity, bias=convb[:, m:m + 1]),
                         [("ps", b), *ck("convb", 4)], [("cvb", cb2)])
                    P.op("act", lambda h, m=m, b=b, cb2=cb2: h.activation(out=csq[:, cb2, :], in_=ps[b][:],
                                                                          func=AF.Square, bias=convb[:, m:m + 1]),
                         [("ps", b), *ck("convb", 4)], [("csq", cb2)])
                    P.op("act", lambda h, m=m, b=b, cb2=cb2: h.activation(out=cv32[:, cb2, :], in_=ps[b][:],
                                                                          func=AF.Identity, bias=convb[:, m:m + 1]),
                         [("ps", b), *ck("convb", 4)], [("cv32", cb2)])
                    _stop("C2")
                    bm = bank()
                    be = bank()
                    P.op("pe", lambda h, bm=bm, cb2=cb2: h.matmul(ps[bm][:], lhsT=blkm[:], rhs=cvb[:, cb2, :],
                                                                  start=True, stop=True),
                         [("cvb", cb2), ("c", "blkm")], [("ps", bm)])
                    P.op("pe", lambda h, be=be, cb2=cb2: h.matmul(ps[be][:], lhsT=blkm[:], rhs=csq[:, cb2, :],
                                                                  start=True, stop=True),
                         [("csq", cb2), ("c", "blkm")], [("ps", be)])
                    _stop("C3")
                    zs = m
                    t2 = 4 + (m % 2) * 3
                    P.op("dve", lambda h, zs=zs, bm=bm, cb2=cb2: h.tensor_tensor(out=Fb[:, zs, :], in0=cv32[:, cb2, :],
                                                                                 in1=ps[bm][:], op=ALU.subtract),
                         [("cv32", cb2), ("ps", bm)], [("F", zs)])
                    _stop("C3_1")
                    P.op("act", lambda h, t2=t2, bm=bm: h.activation(out=Fb[:, t2, :], in_=ps[bm][:], func=AF.Square),
                         [("ps", bm)], [("F", t2)])
                    _stop("C3_2")
                    P.op("dve", lambda h, t2=t2, be=be: h.tensor_tensor(out=Fb[:, t2, :], in0=ps[be][:],
                                                                        in1=Fb[:, t2, :], op=ALU.subtract),
                         [("ps", be), ("F", t2)], [("F", t2)])
                    _stop("C3_3")
                    P.op("act", lambda h, t2=t2: h.activation(out=Fb[:, t2, :], in_=Fb[:, t2, :], func=AF.Ln,
                                                              bias=float(EPS)),
                         [("F", t2)], [("F", t2)])
                    _stop("C3_4")
                    P.op("act", lambda h, t2=t2: h.activation(out=Fb[:, t2, :], in_=Fb[:, t2, :], func=AF.Exp,
                                                              scale=-0.5),
                         [("F", t2)], [("F", t2)])
                    _stop("C3_5")
                    P.op("pool", lambda h, zs=zs, t2=t2: h.tensor_tensor(out=Fb[:, zs, :], in0=Fb[:, zs, :],
                                                                         in1=Fb[:, t2, :], op=ALU.mult),
                         [("F", zs), ("F", t2)], [("F", zs)])
                    _stop("C3_6")
                    P.op("pool", lambda h, zs=zs, m=m: h.tensor_scalar(out=Fb[:, zs, :], in0=Fb[:, zs, :],
                                                                      scalar1=lng[:, m:m + 1], scalar2=lnb[:, m:m + 1],
                                                                      op0=ALU.mult, op1=ALU.add),
                         [("F", zs), *ck("lng", 4), *ck("lnb", 4)], [("F", zs)])
                    _stop("C4")
                    if m == 1:
                        _stop("C5")

                _stop("C")
                for hd in range(4):
                    transposes(lambda c, hd=hd: kT[:, hd, c * 128:(c + 1) * 128], 4,
                               lambda hd=hd: kp[:, :, hd * 128:(hd + 1) * 128],
                               [("kT", hd)], [("kp", hd)], "act")
                kvb = []
                for tt in range(4):
                    bk = bank(hold=True)

                    def fnk(h, tt=tt, bk=bk):
                        r = None
                        for hd in range(4):
                            r = h.matmul(ps[bk][:, hd * 128:(hd + 1) * 128], lhsT=kp[:, tt, hd * 128:(hd + 1) * 128],
                                         rhs=vtok[:, tt, hd * 128:(hd + 1) * 128], start=True, stop=True)
                        return r
                    P.op("pe", fnk, [("kp", hd) for hd in range(4)] + [("v", tt)], [("ps", bk)])
                    kvb.append(bk)
                def s_step(tt):
                    bk = kvb[tt]
                    for hd in range(4):
                        P.op("dve", lambda h, hd=hd, tt=tt: h.tensor_scalar(
                            out=Sst[:, hd, :], in0=Sst[:, hd, :], scalar1=dec[:, hd, tt:tt + 1], scalar2=None,
                            op0=ALU.mult),
                            [("S", hd), ("dec", hd)], [("S", hd)])
                        P.op("dve", lambda h, hd=hd, tt=tt, bk=bk: h.scalar_tensor_tensor(
                            out=Sst[:, hd, :], in0=ps[bk][:, hd * 128:(hd + 1) * 128], scalar=ebl4[:, hd, tt:tt + 1],
                            in1=Sst[:, hd, :], op0=ALU.mult, op1=ALU.add),
                            [("S", hd), ("ebl", hd), ("ps", bk)], [("S", hd)])
                    unhold(bk)

                def s_copy(tt):
                    P.op("act", lambda h, tt=tt: h.activation(out=Sbf[:, tt, :],
                                                              in_=Sst[:].rearrange("p a b -> p (a b)"), func=AF.Copy,
                                                              scale=float(math.exp(-MOFF))),
                         [("S", hd) for hd in range(4)], [("Sbf", tt)])

                obank = {}

                def attn_o(tt):
                    ba = bank()
                    ab = 0

                    def fna(h, tt=tt, ba=ba):
                        r = None
                        for hd in range(4):
                            r = h.matmul(ps[ba][:, hd * 128:(hd + 1) * 128],
                                         lhsT=kT[:, hd, tt * 128:(tt + 1) * 128],
                                         rhs=qT[:, hd, tt * 128:(tt + 1) * 128], start=True, stop=True)
                        return r
                    P.op("pe", fna, [("kT", hd) for hd in range(4)] + [("qT", hd) for hd in range(4)], [("ps", ba)])
                    P.op("dve", lambda h, ba=ba, ab=ab: h.tensor_tensor(
                        out=attnm[:, ab, :].rearrange("p (a b) -> p a b", b=128),
                        in0=ps[ba][:].rearrange("p (a b) -> p a b", b=128),
                        in1=_ap(maskT, 0, [[0, 4], [1, 128]]), op=ALU.mult),
                        [("ps", ba), ("c", "maskT")], [("attnm", ab)])
                def o_part(tt):
                    ab = 0
                    bo = bank(hold=True)
                    obank[tt] = bo
                    sp = (tt - 1) % 4

                    def fno(h, tt=tt, bo=bo, ab=ab, sp=sp):
                        r = None
                        for hd in range(4):
                            h.matmul(ps[bo][:, hd * 128:(hd + 1) * 128], lhsT=qT[:, hd, tt * 128:(tt + 1) * 128],
                                     rhs=Sbf[:, sp, hd * 128:(hd + 1) * 128], start=True, stop=False)
                            r = h.matmul(ps[bo][:, hd * 128:(hd + 1) * 128], lhsT=attnm[:, ab, hd * 128:(hd + 1) * 128],
                                         rhs=vtok[:, tt, hd * 128:(hd + 1) * 128], start=False, stop=True)
                        return r
                    P.op("pe", fno, [("qT", hd) for hd in range(4)] + [("Sbf", sp), ("attnm", ab), ("v", tt)],
                         [("ps", bo)])
                    for hd in range(4):
                        P.op("act", lambda h, hd=hd, tt=tt, bo=bo: h.activation(
                            out=sqj[:, hd * 128:(hd + 1) * 128], in_=ps[bo][:, hd * 128:(hd + 1) * 128], func=AF.Square,
                            accum_out=stat[:, 1, tt * 4 + hd:tt * 4 + hd + 1]),
                            [("ps", bo)], [("stat", 1, tt * 4 + hd), ("sqj",)])

                def o_evac(tt):
                    bo = obank[tt]
                    P.op("dve", lambda h, tt=tt, bo=bo: h.tensor_tensor(out=gsilu[:, tt, :], in0=ps[bo][:],
                                                                        in1=gsilu[:, tt, :], op=ALU.mult),
                         [("ps", bo), ("gsilu", tt)], [("gsilu", tt)])
                    unhold(bo)

                for tt in range(4):
                    s_step(tt)
                    if tt < 3:
                        s_copy(tt)
                attn_o(0)
                conv_chunk(0)
                o_part(0)
                s_copy(3)
                for tt in range(1, 4):
                    attn_o(tt)
                    conv_chunk(tt)
                    o_part(tt)
                    o_evac(tt - 1)
                o_evac(3)
                newton(stat[:, 1, 0:16], 16, 1.0 / 128, stat[:, 2, 0:16],
                       [("stat", 1, i) for i in range(16)], [("stat", 2, "r")])
                for m in range(4):
                    t2 = 4 + (m % 2) * 3
                    P.op("act", lambda h, m=m, t2=t2: h.activation(out=Fb[:, t2, :], in_=Fb[:, m, :], func=AF.Sigmoid),
                         [("F", m)], [("F", t2)])
                    P.op("pool", lambda h, m=m, t2=t2: h.tensor_tensor(out=xT[:, m, :], in0=Fb[:, m, :], in1=Fb[:, t2, :],
                                                                       op=ALU.mult),
                         [("F", m), ("F", t2)], [("xT", hb, m, tt) for tt in range(4)])
                for tt in range(4):
                    yb = 0
                    P.op("dve", lambda h, tt=tt, yb=yb: h.tensor_tensor(
                        out=yrec[:, yb, :].rearrange("p (a b) -> p a b", b=128),
                        in0=gsilu[:, tt, :].rearrange("p (a b) -> p a b", b=128),
                        in1=_ap(stat, 2 * 16 + tt * 4, [[1, 4], [0, 128]]), op=ALU.mult),
                        [("gsilu", tt), ("stat", 2, "r")], [("yrec", yb)])
                    transposes(lambda c, yb=yb: yrec[:, yb, c * 128:(c + 1) * 128], 4,
                               lambda tt=tt: xT[:, 4:8, tt * 128:(tt + 1) * 128],
                               [("yrec", yb)], [("xT", hb, 4 + c, tt) for c in range(4)], "act")

                _stop("D")
                s_o0 = wload(wout_b, 0, 8, 0)
                s_o1 = wload(wout_b, 0, 8, 512)

                def ffn_T(tt):
                    xb = tt % 2
                    transposes(lambda c, xb=xb: xn[:, xb, c * 128:(c + 1) * 128], 8,
                               lambda tt=tt: xT2[:, hb * 8:(hb + 1) * 8, tt * 128:(tt + 1) * 128],
                               [("xn", xb)], [("xT", hb, kc, tt) for kc in range(8)],
                               "act" if tt % 2 else "dve")

                for tt in range(4):
                    for c in range(2):
                        b = bank()
                        mm_tm(s_o0 if c == 0 else s_o1, tt, b)
                        P.op("dve", lambda h, tt=tt, b=b, c=c: h.tensor_tensor(
                            out=hbuf[:, hb, tt, c * 512:(c + 1) * 512], in0=ps[b][:],
                            in1=hbuf[:, hb, tt, c * 512:(c + 1) * 512], op=ALU.add),
                            [("ps", b), ("h", hb, tt)], [("h", hb, tt)])
                    P.op("act", lambda h, tt=tt: h.activation(out=sqj[:], in_=hbuf[:, hb, tt, :], func=AF.Square,
                                                              accum_out=stat[:, 3, tt:tt + 1]),
                         [("h", hb, tt)], [("stat", 3, tt), ("sqj",)])
                    newton(stat[:, 3, tt:tt + 1], 1, 1.0 / D, stat[:, 3, 4 + tt:5 + tt],
                           [("stat", 3, tt)], [("stat", 3, "r", tt)])
                    xb = tt % 2
                    P.op("pool" if tt % 2 == 0 else "dve", lambda h, tt=tt, xb=xb: h.tensor_scalar(
                        out=xn[:, xb, :], in0=hbuf[:, hb, tt, :], scalar1=stat[:, 3, 4 + tt:5 + tt],
                        scalar2=1.0, op0=ALU.mult, op1=ALU.mult),
                        [("h", hb, tt), ("stat", 3, "r", tt)], [("xn", xb)])
                    if tt >= 1:
                        ffn_T(tt - 1)
                ffn_T(3)

                _stop("E")
                s_p = wload(wple_b, 0, 2, 0, ncols=1024)
                pview = ring[:, s_p].rearrange("p k n -> p (k n)")[:, 0:2048].rearrange("p (k n) -> p k n", n=1024)
                for tt in range(4):
                    pbb = tt % 2
                    P.op("pool", lambda h, tt=tt, pbb=pbb: h.tensor_scalar(out=pb[:, pbb, :], in0=pin[:, tt, :],
                                                                          scalar1=1.0, scalar2=1.0, op0=ALU.mult,
                                                                          op1=ALU.mult),
                         [("pin", tt)], [("pb", pbb)])
                    transposes(lambda c, pbb=pbb: pb[:, pbb, c * 128:(c + 1) * 128], 2,
                               lambda tt=tt: pT[:, :, tt * 128:(tt + 1) * 128],
                               [("pb", pbb)], [("pT", tt)], "act")
                for tt in range(4):
                    for c in range(2):
                        b = bank()
                        mm_tm(s_p, tt, b, nkc=2, src=pT, ncols=512, c0=c * 512, slotview=pview)
                        P.op("dve", lambda h, b=b, c=c, tt=tt: h.tensor_copy(out=Fb[:, tt * 2 + c, :], in_=ps[b][:]),
                             [("ps", b)], [("F", tt * 2 + c)])
                        P.op("act", lambda h, b=b, c=c, tt=tt: h.activation(
                            out=sqj[:, 0:512], in_=ps[b][:], func=AF.Square,
                            accum_out=stat[:, 4, tt * 2 + c:tt * 2 + c + 1]),
                            [("ps", b)], [("stat", 4, tt * 2 + c), ("sqj",)])
                P.op("dve", lambda h: h.tensor_tensor(
                    out=stat[:, 5, 0:4], in0=_ap(stat, 4 * 16, [[2, 4]]), in1=_ap(stat, 4 * 16 + 1, [[2, 4]]),
                    op=ALU.add),
                    [("stat", 4, i) for i in range(8)], [("stat", 5, "s")])
                newton(stat[:, 5, 0:4], 4, 1.0 / D, stat[:, 5, 4:8], [("stat", 5, "s")], [("stat", 5, "r")])

                def gple_premult():
                    for tt in range(4):
                        for c in range(2):
                            P.op("pool", lambda h, c=c, tt=tt: h.tensor_tensor(
                                out=Fb[:, tt * 2 + c, :], in0=Fb[:, tt * 2 + c, :],
                                in1=gple_bc[:, c * 512:(c + 1) * 512], op=ALU.mult),
                                [("F", tt * 2 + c), ("c", "gple")], [("F", tt * 2 + c)])
                nxt = st + 1 < NST
                if nxt and st >= 1:
                    norm_stats(0, 1 - hb)
                for a in range(2):
                    for cbk in range(4):
                        s_u = wload(wup_b, 0, 8, a * 2048 + cbk * 512)
                        for m in range(4):
                            b = bank()
                            mm_fm(s_u, m, b)
                            ts = 8 + (m % 2)
                            ach = cbk * 4 + m
                            P.op("act", lambda h, b=b, ts=ts: h.activation(out=Fb[:, ts, :], in_=ps[b][:],
                                                                           func=AF.Square),
                                 [("ps", b)], [("F", ts)])
                            P.op("dve", lambda h, b=b, ts=ts, ach=ach: h.scalar_tensor_tensor(
                                out=aT[:, ach, :], in0=ps[b][:], scalar=0.0, in1=Fb[:, ts, :], op0=ALU.is_gt,
                                op1=ALU.mult),
                                [("ps", b), ("F", ts)], [("aT", ach)])
                        if a == 0 and cbk == 0:
                            gple_premult()
                    for c in range(2):
                        banks4 = [bank(hold=True) for _ in range(4)]
                        for rb in range(2):
                            s_d = wload(wdown_b, a * 2048 + rb * 1024, 8, c * 512)
                            for tt in range(4):
                                b = banks4[tt]

                                def fnd(h, tt=tt, b=b, rb=rb, s_d=s_d):
                                    r = None
                                    for kc in range(8):
                                        r = h.matmul(ps[b][:], lhsT=aT[:, rb * 8 + kc, tt * 128:(tt + 1) * 128],
                                                     rhs=ring[:, s_d, kc, :], start=(rb == 0 and kc == 0),
                                                     stop=(rb == 1 and kc == 7))
                                    return r
                                P.op("pe", fnd, rkeys(s_d) + [("aT", rb * 8 + kc) for kc in range(8)], [("ps", b)])
                        for tt in range(4):
                            b = banks4[tt]
                            P.op("dve", lambda h, tt=tt, b=b, c=c: h.tensor_tensor(
                                out=hbuf[:, hb, tt, c * 512:(c + 1) * 512], in0=ps[b][:],
                                in1=hbuf[:, hb, tt, c * 512:(c + 1) * 512], op=ALU.add),
                                [("ps", b), ("h", hb, tt)], [("h", hb, tt)])
                            unhold(b)

                _stop("F")
                s_g0 = wload(wgate_b, 0, 8, 0)
                s_g1 = wload(wgate_b, 0, 8, 512)
                if st == 0 and NST > 1:
                    load_x(1)
                    norm_stats(0, 1 - hb)
                if nxt:
                    norm_apply(0, 0, 1 - hb)

                def gate_tile(tt):
                    xb = tt % 2
                    P.op("pool", lambda h, tt=tt, xb=xb: h.tensor_scalar(out=xn[:, xb, :], in0=hbuf[:, hb, tt, :],
                                                                        scalar1=1.0, scalar2=1.0, op0=ALU.mult,
                                                                        op1=ALU.mult),
                         [("h", hb, tt)], [("xn", xb)])
                    transposes(lambda c, xb=xb: xn[:, xb, c * 128:(c + 1) * 128], 8,
                               lambda tt=tt: xT[:, :, tt * 128:(tt + 1) * 128],
                               [("xn", xb)], [("xT", hb, kc, tt) for kc in range(8)], "dve" if tt % 2 else "act")
                    for c in range(2):
                        b = bank()
                        mm_tm(s_g0 if c == 0 else s_g1, tt, b)
                        P.op("act", lambda h, b=b, c=c, tt=tt: h.activation(out=gsilu[:, (tt % 2) * 2 + c, :],
                                                                            in_=ps[b][:], func=AF.Sigmoid),
                             [("ps", b)], [("gsilu", (tt % 2) * 2 + c)])

                def combine_tile(tt):
                    for c in range(2):
                        gs = (tt % 2) * 2 + c
                        P.op("dve", lambda h, tt=tt, c=c, gs=gs: h.scalar_tensor_tensor(
                            out=Fb[:, tt * 2 + c, :], in0=Fb[:, tt * 2 + c, :], scalar=stat[:, 5, 4 + tt:5 + tt],
                            in1=gsilu[:, gs, :], op0=ALU.mult, op1=ALU.mult),
                            [("F", tt * 2 + c), ("gsilu", gs), ("stat", 5, "r")], [("F", tt * 2 + c)])
                        P.op("dve", lambda h, tt=tt, c=c: h.tensor_tensor(
                            out=hbuf[:, hb, tt, c * 512:(c + 1) * 512], in0=hbuf[:, hb, tt, c * 512:(c + 1) * 512],
                            in1=Fb[:, tt * 2 + c, :], op=ALU.add),
                            [("h", hb, tt), ("F", tt * 2 + c)], [("h", hb, tt)])
                    P.op("act", lambda h, tt=tt: h.activation(out=sqj[:], in_=hbuf[:, hb, tt, :], func=AF.Square,
                                                              accum_out=stat[:, 6, tt:tt + 1]),
                         [("h", hb, tt)], [("stat", 6, tt), ("sqj",)])

                for tt in range(4):
                    gate_tile(tt)
                    if tt >= 1:
                        combine_tile(tt - 1)
                if nxt:
                    first_pass[0] = False
                    prefetched[("w_in_b", 0, 1536)] = wload(win_b, 0, 8, 1536)
                    prefetched[("w_in_b", 0, 1024)] = wload(win_b, 0, 8, 1024)
                    prefetched[("w_in_b", 0, 512)] = wload(win_b, 0, 8, 512)
                combine_tile(3)
                _stop("G")
                newton(stat[:, 6, 0:4], 4, 1.0 / D, stat[:, 6, 4:8],
                       [("stat", 6, tt) for tt in range(4)], [("stat", 6, "r")])
                for tt in range(4):
                    P.op("dve", lambda h, tt=tt: h.scalar_tensor_tensor(
                        out=hbuf[:, hb, tt, :], in0=hbuf[:, hb, tt, :], scalar=stat[:, 6, 4 + tt:5 + tt],
                        in1=gfin_bc[:], op0=ALU.mult, op1=ALU.mult),
                        [("h", hb, tt), ("stat", 6, "r"), ("c", "gfin")], [("h", hb, tt)])
                    r0 = st * T + tt * 128
                    pending_stores.append((r0, hb, tt))
                if st + 1 >= NST:
                    deferred_stores()

            for st in range(NST):
                do_st(st)
        except _Stop:
            pass
        P.emit()
    return nc


_NC_CACHE = {}


def kernel(x, p, norm_mix_g, w_in, conv_w, conv_b, conv_ln_g, conv_ln_b, lb_logits, rec_norm_g, w_out,
           norm_ffn_g, w_up, w_down, w_ple, ple_norm_g, w_ple_gate, final_norm_g):
    f = lambda a: np.ascontiguousarray(np.asarray(a, dtype=np.float32))
    x = f(x)
    p = f(p)
    B, S, _ = x.shape
    if S not in _NC_CACHE:
        _NC_CACHE[S] = build(S)
    nc = _NC_CACHE[S]
    shared = {
        "norm_mix_g": f(norm_mix_g)[0], "conv_w": f(conv_w)[0], "conv_b": f(conv_b)[0],
        "conv_ln_g": f(conv_ln_g)[0], "conv_ln_b": f(conv_ln_b)[0], "lb_logits": f(lb_logits),
        "rec_norm_g": f(rec_norm_g)[0], "norm_ffn_g": f(norm_ffn_g)[0], "ple_norm_g": f(ple_norm_g)[0],
        "final_norm_g": f(final_norm_g),
        "wf32": pack_weights({"w_in": f(w_in)[0], "w_out": f(w_out)[0], "w_up": f(w_up)[0], "w_down": f(w_down)[0],
                              "w_ple": f(w_ple)[0], "w_ple_gate": f(w_ple_gate)[0]}),
    }
    in_maps = []
    for b in range(B):
        m = dict(shared)
        m["x"] = x[b]
        m["p"] = p[0, b]
        in_maps.append(m)
    res = run_bass_kernel_spmd(nc, in_maps, core_ids=list(range(B)))
    return np.stack([np.asarray(r["out"], dtype=np.float32) for r in res.results], axis=0)
```
